# Optimizing a Trainium2 kernel written in Bass

```python
import jax, jax.numpy as jnp
from jax import lax
import numpy as np

D_MODEL = 2048
BATCH = 8
SEQ = 2048
DEPTH = 2

HEAD_DIM = 64
D_FF = 256 * ((8 * D_MODEL // 3 + 255) // 256)
ATTN_BLOCK = 128
SWA_HEADS = D_MODEL // (4 * HEAD_DIM)
SWA_KV_HEADS = 2
SWA_WINDOW = 128
NSA_HEADS = D_MODEL // (4 * HEAD_DIM)
NSA_KV_HEADS = 2
NSA_CMP_LEN = 32
NSA_CMP_STRIDE = 16
NSA_CMP_HIDDEN = 256
NSA_SEL_BLOCK = 64
NSA_TOP_N = 16
NSA_WINDOW = 512
NSA_Q_CHUNK = 64
RNN_WIDTH = D_MODEL // 2
RNN_BLOCKS = 16
RNN_BLOCK_WIDTH = RNN_WIDTH // RNN_BLOCKS
CONV_WIDTH = 4
LRU_C = 8.0
D_SWA = SWA_HEADS * HEAD_DIM
D_NSA = NSA_HEADS * HEAD_DIM
D_MIX = D_SWA + D_NSA + RNN_WIDTH
SWA_KVD = SWA_KV_HEADS * HEAD_DIM
NSA_KVD = NSA_KV_HEADS * HEAD_DIM
IN_SPLITS = (D_SWA, SWA_KVD, SWA_KVD,
             D_NSA, NSA_KVD, NSA_KVD, NSA_KVD, NSA_KVD, NSA_KVD, NSA_KVD, 3 * NSA_HEADS,
             RNN_WIDTH, RNN_WIDTH)
D_IN = sum(IN_SPLITS)
RMS_EPS = 1e-6
NEG_INF = -1e30
FORCED_SCORE = 1e4

kernel_name = "hybrid_swa_nsa_rglru_macaron"


def rmsnorm(x, g):
    xf = x.astype(jnp.float32)
    y = xf * lax.rsqrt(jnp.mean(xf * xf, axis=-1, keepdims=True) + RMS_EPS)
    return (y * g.astype(jnp.float32)).astype(x.dtype)


def swiglu(x, wg, wu, wd):
    return (jax.nn.silu(x @ wg) * (x @ wu)) @ wd


def banded_window_attention(q, k, v, window, sink=None):
    b, s, kvh, g, dh = q.shape
    blk = ATTN_BLOCK
    nb = s // blk
    n_prev = -(-(window - 1) // blk)
    span = (n_prev + 1) * blk

    def band(t):
        tp = jnp.pad(t, ((0, 0), (n_prev * blk, 0), (0, 0), (0, 0)))
        tp = tp.reshape(b, nb + n_prev, blk, kvh, dh)
        return jnp.concatenate([tp[:, j:j + nb] for j in range(n_prev + 1)], axis=2)

    kb, vb = band(k), band(v)
    qb = q.reshape(b, nb, blk, kvh, g, dh)
    scores = jnp.einsum('bnqhgd,bnkhd->bnhgqk', qb, kb).astype(jnp.float32) * (dh ** -0.5)
    qpos = jnp.arange(nb)[:, None, None] * blk + jnp.arange(blk)[None, :, None]
    kpos = (jnp.arange(nb)[:, None, None] - n_prev) * blk + jnp.arange(span)[None, None, :]
    diff = qpos - kpos
    mask = (diff >= 0) & (diff < window) & (kpos >= 0)
    scores = jnp.where(mask[None, :, None, None], scores, NEG_INF)
    m = jnp.max(scores, axis=-1, keepdims=True)
    if sink is not None:
        sk = sink.astype(jnp.float32).reshape(1, 1, kvh, g, 1, 1)
        m = jnp.maximum(m, sk)
        p = jnp.exp(scores - m)
        denom = jnp.sum(p, axis=-1, keepdims=True) + jnp.exp(sk - m)
    else:
        p = jnp.exp(scores - m)
        denom = jnp.sum(p, axis=-1, keepdims=True)
    p = p / denom
    out = jnp.einsum('bnhgqk,bnkhd->bnqhgd', p, vb.astype(jnp.float32))
    return out.reshape(b, s, kvh, g, dh)


def nsa_attention(q, k_cmp, v_cmp, k_slc, v_slc, k_win, v_win, gate_logits,
                  cmp_pos, cmp_w1, cmp_b1, cmp_w2):
    b, s, kvh, g, dh = q.shape
    scale = dh ** -0.5
    pos = jnp.arange(s)

    n_c = (s - NSA_CMP_LEN) // NSA_CMP_STRIDE + 1
    tok_idx = np.arange(n_c)[:, None] * NSA_CMP_STRIDE + np.arange(NSA_CMP_LEN)[None, :]

    def compress(t, i):
        blocks = t[:, tok_idx] + cmp_pos[i][None, None, :, None, :]
        flat = blocks.transpose(0, 1, 3, 2, 4).reshape(b, n_c, kvh, NSA_CMP_LEN * dh)
        return jax.nn.gelu(flat @ cmp_w1[i] + cmp_b1[i]) @ cmp_w2[i]

    kc = compress(k_cmp, 0)
    vc = compress(v_cmp, 1)
    cmp_end = jnp.arange(n_c) * NSA_CMP_STRIDE + NSA_CMP_LEN - 1
    cmask = (cmp_end[None, :] <= pos[:, None])[None, :, None, None, :]
    sc = jnp.einsum('bshgd,bchd->bshgc', q, kc).astype(jnp.float32) * scale
    sc = jnp.where(cmask, sc, NEG_INF)
    p_cmp = jax.nn.softmax(sc, axis=-1) * cmask
    o_cmp = jnp.einsum('bshgc,bchd->bshgd', p_cmp, vc.astype(jnp.float32))

    n_sel = s // NSA_SEL_BLOCK
    cs = np.arange(n_c)[:, None] * NSA_CMP_STRIDE
    ss = np.arange(n_sel)[None, :] * NSA_SEL_BLOCK
    overlap = np.clip(np.minimum(cs + NSA_CMP_LEN, ss + NSA_SEL_BLOCK) - np.maximum(cs, ss), 0, None)
    overlap = jnp.asarray(overlap / NSA_CMP_LEN, jnp.float32)
    imp = jnp.einsum('bshc,cj->bshj', jnp.sum(p_cmp, axis=3), overlap)
    blk_id = jnp.arange(n_sel)[None, :]
    cur = (pos // NSA_SEL_BLOCK)[:, None]
    forced = (blk_id == 0) | (blk_id == cur) | (blk_id == cur - 1)
    valid = blk_id * NSA_SEL_BLOCK <= pos[:, None]
    imp = jnp.where(forced[None, :, None, :], FORCED_SCORE, imp)
    imp = jnp.where(valid[None, :, None, :], imp, NEG_INF)
    n_top = min(NSA_TOP_N, n_sel)
    _, sel_idx = lax.top_k(imp, n_top)

    kb = k_slc.reshape(b, n_sel, NSA_SEL_BLOCK, kvh, dh).transpose(0, 3, 1, 2, 4)
    vb = v_slc.reshape(b, n_sel, NSA_SEL_BLOCK, kvh, dh).transpose(0, 3, 1, 2, 4)
    bi = jnp.arange(b)[:, None, None, None]
    hi = jnp.arange(kvh)[None, None, :, None]
    nq = s // NSA_Q_CHUNK

    def sel_chunk(args):
        qc, ic, pc = args
        kg = kb[bi, hi, ic]
        vg = vb[bi, hi, ic]
        c = qc.shape[1]
        scs = jnp.einsum('bchgd,bchnld->bchgnl', qc, kg).astype(jnp.float32) * scale
        kpos = ic[..., None] * NSA_SEL_BLOCK + jnp.arange(NSA_SEL_BLOCK)
        km = (kpos <= pc[None, :, None, None, None])[:, :, :, None]
        scs = jnp.where(km, scs, NEG_INF).reshape(b, c, kvh, g, n_top * NSA_SEL_BLOCK)
        ps = jax.nn.softmax(scs, axis=-1).reshape(b, c, kvh, g, n_top, NSA_SEL_BLOCK)
        return jnp.einsum('bchgnl,bchnld->bchgd', ps, vg.astype(jnp.float32))

    def chunks(t):
        return t.reshape(b, nq, NSA_Q_CHUNK, *t.shape[2:]).swapaxes(0, 1)

    o_slc = lax.map(sel_chunk, (chunks(q), chunks(sel_idx), pos.reshape(nq, NSA_Q_CHUNK)))
    o_slc = o_slc.swapaxes(0, 1).reshape(b, s, kvh, g, dh)

    o_win = banded_window_attention(q, k_win, v_win, NSA_WINDOW)

    gw = jax.nn.sigmoid(gate_logits.astype(jnp.float32)).reshape(b, s, 3, kvh, g)[..., None]
    return gw[:, :, 0] * o_cmp + gw[:, :, 1] * o_slc + gw[:, :, 2] * o_win


def rglru_block(xr, xg, conv_w, conv_b, wa, ba, wx, bx, lam):
    b, s, c = xr.shape
    xc = lax.conv_general_dilated(xr, conv_w[:, None, :], (1,), [(CONV_WIDTH - 1, 0)],
                                  dimension_numbers=('NWC', 'WIO', 'NWC'),
                                  feature_group_count=c) + conv_b
    xblk = xc.reshape(b, s, RNN_BLOCKS, RNN_BLOCK_WIDTH)
    r = jax.nn.sigmoid((jnp.einsum('bsnc,ncd->bsnd', xblk, wa).reshape(b, s, c) + ba).astype(jnp.float32))
    i = jax.nn.sigmoid((jnp.einsum('bsnc,ncd->bsnd', xblk, wx).reshape(b, s, c) + bx).astype(jnp.float32))
    log_a = -LRU_C * r * jax.nn.softplus(-lam.astype(jnp.float32))
    a = jnp.exp(log_a)
    u = jnp.sqrt(-jnp.expm1(2.0 * log_a)) * (i * xc.astype(jnp.float32))

    def combine(left, right):
        a1, b1 = left
        a2, b2 = right
        return a1 * a2, a2 * b1 + b2

    _, h = lax.associative_scan(combine, (a, u), axis=1)
    return (jax.nn.gelu(xg.astype(jnp.float32)) * h).astype(xr.dtype)


def setup_inputs(seed: int = 0) -> dict:
    key = jax.random.key(seed)
    ks = iter(jax.random.split(key, 40))

    def nrm(shape, scale):
        return jax.random.normal(next(ks), shape, jnp.float32) * scale

    def gain(shape):
        return 1.0 + 0.01 * jax.random.normal(next(ks), shape, jnp.float32)

    a0 = jax.random.uniform(next(ks), (DEPTH, RNN_WIDTH), jnp.float32, 0.9, 0.999)
    base = a0 ** (1.0 / LRU_C)
    lru_lambda = jnp.log(base) - jnp.log1p(-base)
    return {
        "x": jax.random.normal(next(ks), (BATCH, SEQ, D_MODEL), jnp.float32),
        "ffn1_norm": gain((DEPTH, D_MODEL)),
        "ffn1_w_gate": nrm((DEPTH, D_MODEL, D_FF), D_MODEL ** -0.5),
        "ffn1_w_up": nrm((DEPTH, D_MODEL, D_FF), D_MODEL ** -0.5),
        "ffn1_w_down": nrm((DEPTH, D_FF, D_MODEL), D_FF ** -0.5),
        "mix_norm": gain((DEPTH, D_MODEL)),
        "w_in": nrm((DEPTH, D_MODEL, D_IN), D_MODEL ** -0.5),
        "swa_sinks": nrm((DEPTH, SWA_HEADS), 0.5),
        "cmp_pos": nrm((DEPTH, 2, NSA_CMP_LEN, HEAD_DIM), 0.5),
        "cmp_w1": nrm((DEPTH, 2, NSA_CMP_LEN * HEAD_DIM, NSA_CMP_HIDDEN), (NSA_CMP_LEN * HEAD_DIM) ** -0.5),
        "cmp_b1": nrm((DEPTH, 2, NSA_CMP_HIDDEN), 0.01),
        "cmp_w2": nrm((DEPTH, 2, NSA_CMP_HIDDEN, HEAD_DIM), NSA_CMP_HIDDEN ** -0.5),
        "conv_w": nrm((DEPTH, CONV_WIDTH, RNN_WIDTH), CONV_WIDTH ** -0.5),
        "conv_b": nrm((DEPTH, RNN_WIDTH), 0.01),
        "lru_wa": nrm((DEPTH, RNN_BLOCKS, RNN_BLOCK_WIDTH, RNN_BLOCK_WIDTH), RNN_BLOCK_WIDTH ** -0.5),
        "lru_ba": nrm((DEPTH, RNN_WIDTH), 0.01),
        "lru_wx": nrm((DEPTH, RNN_BLOCKS, RNN_BLOCK_WIDTH, RNN_BLOCK_WIDTH), RNN_BLOCK_WIDTH ** -0.5),
        "lru_bx": nrm((DEPTH, RNN_WIDTH), 0.01),
        "lru_lambda": lru_lambda,
        "group_norm": gain((DEPTH, D_MIX)),
        "w_out": nrm((DEPTH, D_MIX, D_MODEL), D_MIX ** -0.5),
        "ffn2_norm": gain((DEPTH, D_MODEL)),
        "ffn2_w_gate": nrm((DEPTH, D_MODEL, D_FF), D_MODEL ** -0.5),
        "ffn2_w_up": nrm((DEPTH, D_MODEL, D_FF), D_MODEL ** -0.5),
        "ffn2_w_down": nrm((DEPTH, D_FF, D_MODEL), D_FF ** -0.5),
        "final_norm": gain((D_MODEL,)),
    }


def reference(x, ffn1_norm, ffn1_w_gate, ffn1_w_up, ffn1_w_down, mix_norm, w_in, swa_sinks,
              cmp_pos, cmp_w1, cmp_b1, cmp_w2, conv_w, conv_b, lru_wa, lru_ba, lru_wx, lru_bx,
              lru_lambda, group_norm, w_out, ffn2_norm, ffn2_w_gate, ffn2_w_up, ffn2_w_down,
              final_norm):
    b, s, _ = x.shape
    split_at = [int(v) for v in np.cumsum(IN_SPLITS)[:-1]]
    ga = SWA_HEADS // SWA_KV_HEADS
    gb = NSA_HEADS // NSA_KV_HEADS
    for l in range(DEPTH):
        x = x + 0.5 * swiglu(rmsnorm(x, ffn1_norm[l]), ffn1_w_gate[l], ffn1_w_up[l], ffn1_w_down[l])

        h = rmsnorm(x, mix_norm[l])
        proj = h @ w_in[l]
        (qa, ka, va, qb, kcb, vcb, ksb, vsb, kwb, vwb, gl, xr, xg) = jnp.split(proj, split_at, axis=-1)

        def kvh(t, n):
            return t.reshape(b, s, n, HEAD_DIM)

        ya = banded_window_attention(qa.reshape(b, s, SWA_KV_HEADS, ga, HEAD_DIM),
                                     kvh(ka, SWA_KV_HEADS), kvh(va, SWA_KV_HEADS),
                                     SWA_WINDOW, swa_sinks[l]).reshape(b, s, D_SWA)
        yb = nsa_attention(qb.reshape(b, s, NSA_KV_HEADS, gb, HEAD_DIM),
                           kvh(kcb, NSA_KV_HEADS), kvh(vcb, NSA_KV_HEADS),
                           kvh(ksb, NSA_KV_HEADS), kvh(vsb, NSA_KV_HEADS),
                           kvh(kwb, NSA_KV_HEADS), kvh(vwb, NSA_KV_HEADS), gl,
                           cmp_pos[l], cmp_w1[l], cmp_b1[l], cmp_w2[l]).reshape(b, s, D_NSA)
        yc = rglru_block(xr, xg, conv_w[l], conv_b[l], lru_wa[l], lru_ba[l], lru_wx[l], lru_bx[l],
                         lru_lambda[l])

        gn = group_norm[l]
        y = jnp.concatenate([
            rmsnorm(ya.astype(x.dtype), gn[:D_SWA]),
            rmsnorm(yb.astype(x.dtype), gn[D_SWA:D_SWA + D_NSA]),
            rmsnorm(yc, gn[D_SWA + D_NSA:]),
        ], axis=-1)
        x = x + y @ w_out[l]

        x = x + 0.5 * swiglu(rmsnorm(x, ffn2_norm[l]), ffn2_w_gate[l], ffn2_w_up[l], ffn2_w_down[l])
    return rmsnorm(x, final_norm)
```

```python
import numpy as np
from contextlib import ExitStack
import concourse.bass as bass
import concourse.mybir as mybir
from concourse.bass_utils import run_bass_kernel_spmd

F32 = mybir.dt.float32
BF16 = mybir.dt.bfloat16
I32 = mybir.dt.int32
AF = mybir.ActivationFunctionType
ALU = mybir.AluOpType
AX = mybir.AxisListType

D = 2048
S = 2048
DEPTH = 2
DFF = 5632
NFF = DFF // 128
DIN = 4120
RMS_EPS = 1e-6
SEM_CAP = 30000


class _Op:
    __slots__ = ("eng", "fn", "r", "w", "dma", "deps", "awaited", "sem", "semval")

    def __init__(self, eng, fn, r, w, dma):
        self.eng = eng
        self.fn = fn
        self.r = tuple(r)
        self.w = tuple(w)
        self.dma = dma
        self.deps = None
        self.awaited = False
        self.sem = None
        self.semval = 0


class Sched:
    def __init__(self, nc, es, n_dma_sems=20):
        self.nc = nc
        self.es = es
        self.ops = []
        self.ENG = {"pe": nc.tensor, "act": nc.scalar, "dve": nc.vector,
                    "pool": nc.gpsimd, "sp": nc.sync}
        self.n_dma_sems = n_dma_sems
        self._semh = {}
        self.cur_barrier = None

    def add(self, eng, fn, r=(), w=(), dma=False):
        w = list(w) + [k for k in r if k.startswith("bk")]
        r = [k for k in r if not k.startswith("bk")]
        op = _Op(eng, fn, r, w, dma)
        op.deps = set()
        if self.cur_barrier is not None:
            op.deps.add(self.cur_barrier)
        self.ops.append(op)

    def barrier(self, fn):
        op = _Op("dve", fn, (), (), False)
        op.deps = set()
        start = self.cur_barrier if self.cur_barrier is not None else 0
        last = {}
        for i in range(start, len(self.ops)):
            o = self.ops[i]
            if o.dma:
                op.deps.add(i)
            else:
                last[o.eng] = i
        op.deps.update(last.values())
        self.ops.append(op)
        self.cur_barrier = len(self.ops) - 1

    def _sem(self, name):
        if name not in self._semh:
            self._semh[name] = self.es.enter_context(self.nc.semaphore(name))
        return self._semh[name]

    def emit(self):
        ops = self.ops
        last_w, rd_eng, rd_dma = {}, {}, {}
        for i, op in enumerate(ops):
            deps = op.deps
            for k in op.r:
                if k in last_w:
                    deps.add(last_w[k])
            for k in op.w:
                if k in last_w:
                    deps.add(last_w[k])
                deps.update(rd_eng.get(k, {}).values())
                deps.update(rd_dma.get(k, ()))
            deps.discard(i)
            op.deps = deps
            for k in op.r:
                if op.dma:
                    rd_dma.setdefault(k, []).append(i)
                else:
                    rd_eng.setdefault(k, {})[op.eng] = i
            for k in op.w:
                last_w[k] = i
                rd_eng[k] = {}
                rd_dma[k] = []
        rr, last_user, dcount = {}, {}, {}
        for i, op in enumerate(ops):
            if op.dma:
                n = rr.get(op.eng, 0)
                rr[op.eng] = n + 1
                s = "d_%s_%d" % (op.eng, n % self.n_dma_sems)
                if s in last_user:
                    op.deps.add(last_user[s])
                last_user[s] = i
                dcount[s] = dcount.get(s, 0) + 1
                op.sem = s
                op.semval = 16 * dcount[s]
        for op in ops:
            for j in op.deps:
                d = ops[j]
                if d.dma:
                    continue
                if d.eng == op.eng and op.eng == "pe" and not op.dma:
                    continue
                d.awaited = True
        cnt = {}
        for op in ops:
            if not op.dma and op.awaited:
                c = cnt.get(op.eng, 0)
                cnt[op.eng] = c + 1
                op.sem = "e_%s_%d" % (op.eng, c // SEM_CAP)
                op.semval = c % SEM_CAP + 1
        waited = {e: {} for e in self.ENG}
        nwait = 0
        for op in ops:
            e = op.eng
            need = {}
            for j in op.deps:
                d = ops[j]
                if (not d.dma) and d.eng == e and e == "pe" and not op.dma:
                    continue
                if need.get(d.sem, 0) < d.semval:
                    need[d.sem] = d.semval
            for s, v in need.items():
                if waited[e].get(s, 0) < v:
                    self.ENG[e].wait_ge(self._sem(s), v)
                    waited[e][s] = v
                    nwait += 1
            if op.fn is None:
                continue
            ins = op.fn()
            if op.dma:
                ins.then_inc(self._sem(op.sem), 16)
            elif op.awaited:
                ins.then_inc(self._sem(op.sem), 1)
        self.stats = dict(n_ops=len(ops), n_wait=nwait, awaited=dict(cnt))


class Builder:
    def __init__(self):
        self.nc = bass.Bass("TRN2", target_bir_lowering=False)
        self.es = ExitStack()
        self.S = Sched(self.nc, self.es)
        self._n = 0
        self.rr = {}
        self.bar_tile = self.sb("bar_tile", [128, 1], F32)

    def sb(self, name, shape, dt):
        return self.es.enter_context(self.nc.sbuf_tensor("s_" + name, list(shape), dt))

    def ps(self, name, shape, dt):
        return self.es.enter_context(self.nc.psum_tensor("p_" + name, list(shape), dt))

    def dram(self, name, shape, dt, kind="Internal"):
        return self.nc.dram_tensor("d_" + name, list(shape), dt, kind=kind).ap()

    def dma(self, q, out, in_, r, w, **kw):
        eng = self.S.ENG[q]
        self.S.add(q, lambda: eng.dma_start(out=out, in_=in_, **kw), r, w, dma=True)

    def mm(self, out, lhsT, rhs, start, stop, r, w):
        nc = self.nc
        self.S.add("pe", lambda: nc.tensor.matmul(out, lhsT, rhs, start=start, stop=stop), r, w)

    def tr(self, out, in_, ident, r, w):
        nc = self.nc
        self.S.add("pe", lambda: nc.tensor.transpose(out, in_, ident), r, w)

    def act(self, out, in_, func, r, w, bias=None, scale=None, accum_out=None):
        nc = self.nc
        kw = {}
        if bias is not None:
            kw["bias"] = bias
        if scale is not None:
            kw["scale"] = scale
        if accum_out is not None:
            kw["accum_out"] = accum_out
        self.S.add("act", lambda: nc.scalar.activation(out=out, in_=in_, func=func, **kw), r, w)

    def v(self, eng, name, r, w, *a, **kw):
        e = self.S.ENG[eng]
        self.S.add(eng, lambda: getattr(e, name)(*a, **kw), r, w)

    def barrier(self):
        nc = self.nc
        bt = self.bar_tile
        self.S.barrier(lambda: nc.vector.memset(bt[:], 0.0))

    def alt(self, key, choices):
        n = self.rr.get(key, 0)
        self.rr[key] = n + 1
        return choices[n % len(choices)]


NEG = -30000.0
ARENA_BYTES = 143360
SC_BANKS = [0, 1, 2]
NPV = 76


class Arena:
    def __init__(self, B):
        self.t = B.sb("arena", [128, ARENA_BYTES // 4], F32)

    def view(self, off, shape, dt, nparts=128):
        esz = 4 if dt == F32 else 2
        n = 1
        for d in shape:
            n *= d
        assert off % 4 == 0 and (n * esz) % 4 == 0 and off + n * esz <= ARENA_BYTES, (off, shape)
        v = self.t[0:nparts, off // 4:(off + n * esz) // 4]
        if dt != F32:
            v = v.bitcast(dt)
        if len(shape) == 2:
            v = v.rearrange("p (a b) -> p a b", a=shape[0])
        elif len(shape) == 3:
            v = v.rearrange("p (a b c) -> p a b c", a=shape[0], b=shape[1])
        return v


def norm_T(B, C, xt, kx, width, g_bc, kg, dstT, kdst, slot, transpose=True):
    junk, ssq, rstd, xn = C["junk"], C["ssq"][slot], C["rstd"][slot], C["xn"][slot]
    kj, ks, kr, kn = "junk", "ssq%d" % slot, "rstd%d" % slot, "xn%d" % slot
    B.act(junk[:, 0:width], xt, AF.Square, [kx], [kj, ks], accum_out=ssq[:])
    B.act(rstd[:], ssq[:], AF.Sqrt, [ks, "epsc"], [kr], bias=C["eps"][:], scale=1.0 / width)
    B.v("dve", "reciprocal", [kr], [kr], out=rstd[:], in_=rstd[:])
    B.v("dve", "scalar_tensor_tensor", [kx, kr, kg], [kn], out=xn[:, 0:width], in0=xt, scalar=rstd[:],
        in1=g_bc, op0=ALU.mult, op1=ALU.mult)
    if not transpose:
        return xn, kn
    for q4 in range(width // 512):
        pb = B.alt("trbank", [6, 7])
        pt = C["banks"][pb]
        kp = "bk%d" % pb
        for j in range(4):
            kc = q4 * 4 + j
            B.tr(pt[:, j * 128:(j + 1) * 128], xn[:, kc * 128:(kc + 1) * 128], C["identf"][:],
                 [kn, "ident"], [kp])
        eng = B.alt("trcopy", ["act", "dve"])
        src3 = pt[:].rearrange("p (j c) -> p j c", j=4)
        if eng == "act":
            B.act(dstT[:, q4 * 4:(q4 + 1) * 4, :], src3, AF.Copy, [kp], [kdst])
        else:
            B.v("dve", "tensor_copy", [kp], [kdst], out=dstT[:, q4 * 4:(q4 + 1) * 4, :], in_=src3)


def load_gain(B, C, gvec, slot):
    n = gvec.shape[1]
    B.dma("sp", C["gbc"][slot][:, 0:n], gvec.partition_broadcast(128), [], ["gbc%d" % slot])
    return C["gbc"][slot], "gbc%d" % slot


def ffn_chunk(B, C, src, dst, ksrc, kdst, g_bc, kg, wg, wu, wd, uid):
    A = C["arena"]
    xnT = A.view(0, [16, 512], BF16)
    act = A.view(16384, [NFF, 512], BF16)
    wgs = [A.view(61440 + i * 8192, [16, 256], BF16) for i in range(3)]
    wus = [A.view(86016 + i * 8192, [16, 256], BF16) for i in range(3)]
    wds = [A.view(110592 + i * 4096, [4, 512], BF16) for i in range(3)]
    sgs = [A.view(122880 + i * 2048, [512], F32) for i in range(2)]
    xss = [A.view(126976 + i * 2048, [512], F32) for i in range(4)]
    oss = [A.view(135168 + i * 2048, [512], F32) for i in range(4)]
    for t in range(4):
        xt = C["xt"][t % 2]
        kx = "xt%d" % (t % 2)
        B.dma("sp", xt[:], src[t * 128:(t + 1) * 128, :], [ksrc], [kx])
        norm_T(B, C, xt[:], kx, 2048, g_bc[:, 0:2048], kg, xnT[:, :, t * 128:(t + 1) * 128], "xnT", t % 2)
    wgv = wg.rearrange("(kc p) f -> p kc f", p=128)
    wuv = wu.rearrange("(kc p) f -> p kc f", p=128)
    for sl in range(NFF // 2):
        n = B.alt("wgu", [0, 1, 2])
        kwg, kwu = "wgs%d" % n, "wus%d" % n
        B.dma("pool", wgs[n], wgv[:, :, sl * 256:(sl + 1) * 256], ["W" + uid], [kwg])
        B.dma("pool", wus[n], wuv[:, :, sl * 256:(sl + 1) * 256], ["W" + uid], [kwu])
        for j in range(2):
            ff = sl * 2 + j
            pb = B.alt("gubank", [0, 1])
            gps, ups = C["banks"][2 * pb], C["banks"][2 * pb + 1]
            kgp, kup = "bk%d" % (2 * pb), "bk%d" % (2 * pb + 1)
            for kc in range(16):
                B.mm(gps[:], wgs[n][:, kc, j * 128:(j + 1) * 128], xnT[:, kc, :], kc == 0, kc == 15,
                     [kwg, "xnT"], [kgp])
            for kc in range(16):
                B.mm(ups[:], wus[n][:, kc, j * 128:(j + 1) * 128], xnT[:, kc, :], kc == 0, kc == 15,
                     [kwu, "xnT"], [kup])
            ksg = "sg%d" % pb
            B.act(sgs[pb], gps[:], AF.Silu, [kgp], [ksg])
            B.v("dve", "tensor_tensor", [ksg, kup], ["act%d" % ff], out=act[:, ff, :], in0=sgs[pb],
                in1=ups[:], op=ALU.mult)
    wdv = wd.rearrange("(f p) d -> p f d", p=128)
    for s in range(4):
        accs = C["banks"][4:8]
        for fg in range(NFF // 4):
            n = B.alt("wd", [0, 1, 2])
            kwd = "wds%d" % n
            B.dma("pool", wds[n], wdv[:, fg * 4:(fg + 1) * 4, s * 512:(s + 1) * 512], ["W" + uid], [kwd])
            for j in range(4):
                ff = fg * 4 + j
                for tt in range(4):
                    B.mm(accs[tt][:], act[:, ff, tt * 128:(tt + 1) * 128], wds[n][:, j, :], ff == 0,
                         ff == NFF - 1, [kwd, "act%d" % ff], ["bk%d" % (4 + tt)])
        for tt in range(4):
            n = B.alt("xs", [0, 1, 2, 3])
            kxs, kos = "xs%d" % n, "os%d" % n
            B.dma("sp", xss[n], src[tt * 128:(tt + 1) * 128, s * 512:(s + 1) * 512], [ksrc], [kxs])
            B.v("dve", "scalar_tensor_tensor", ["bk%d" % (4 + tt), kxs], [kos], out=oss[n],
                in0=accs[tt][:], scalar=0.5, in1=xss[n], op0=ALU.mult, op1=ALU.add)
            B.dma("sp", dst[tt * 128:(tt + 1) * 128, s * 512:(s + 1) * 512], oss[n], [kos], [kdst])


def attn_run(B, C, blocks, obank, kob):
    groups = [blocks[i:i + 4] for i in range(0, len(blocks), 4)]
    pTs = C["pT"]

    def scores(gi):
        sbk = B.alt("scbank", SC_BANKS)
        bank = C["banks"][sbk]
        kb = "bk%d" % sbk
        grp = groups[gi]
        mms = []
        for bi, b in enumerate(grp):
            cs = slice(bi * 128, (bi + 1) * 128)
            mms.append((bank[:, cs], b["kT"], b["qT"], b["rk"]))
            for (l, r, rk) in b["extras"]:
                mms.append((bank[:, cs], l, r, rk))
        for i, (o, l, r, rk) in enumerate(mms):
            B.mm(o, l, r, i == 0, i == len(mms) - 1, rk, [kb])
        pi = B.alt("pT", [0, 1, 2])
        kp = "pT%d" % pi
        n = len(grp) * 128
        B.act(pTs[pi][:, 0:n], bank[:, 0:n], AF.Exp, [kb], [kp], scale=0.125)
        return pTs[pi], kp

    nblk = len(blocks)
    done = [0]

    def pv(gi, pT, kp):
        for bi, b in enumerate(groups[gi]):
            c0, ncol = b["oc"]
            B.mm(obank[:, c0:c0 + ncol], pT[:, bi * 128:(bi + 1) * 128], b["v"], done[0] == 0,
                 done[0] == nblk - 1, [kp] + b["rk"], [kob])
            done[0] += 1

    cur = scores(0)
    for gi in range(len(groups)):
        nxt = scores(gi + 1) if gi + 1 < len(groups) else None
        pv(gi, cur[0], cur[1])
        cur = nxt


def gelu_tanh(B, eng_keys, x, kx, tmp, kt, out, kout):
    B.v("dve", "tensor_tensor", [kx], [kt], out=tmp, in0=x, in1=x, op=ALU.mult)
    B.v("dve", "tensor_scalar", [kt], [kt], out=tmp, in0=tmp, scalar1=0.044715, scalar2=1.0,
        op0=ALU.mult, op1=ALU.add)
    B.v("dve", "tensor_tensor", [kt, kx], [kt], out=tmp, in0=tmp, in1=x, op=ALU.mult)
    B.act(tmp, tmp, AF.Sigmoid, [kt], [kt], scale=1.5957691216057308)
    B.v("dve", "tensor_tensor", [kt, kx], [kout], out=out, in0=tmp, in1=x, op=ALU.mult)


def mixer(B, C, L, src, dst, ksrc, kdst, W, K, SCR, stop=99, dbg=None):
    A = C["arena"]
    banks = C["banks"]
    uid = "m%d" % L
    vE = A.view(0, [16, 6, 66], BF16)
    gl = A.view(12672, [16, 24], F32)
    hT = A.view(14208, [16, 2048], BF16)
    yT = hT
    WS = 79744
    qkT, rgT = SCR["qkT"], SCR["rgT"]

    if stop <= -1:
        return
    pv_ = C["pvec"]
    B.dma("sp", pv_[:], W["pvec"][L], [], ["pvec"])
    B.dma("sp", C["esink"][:], W["sinks"][L], [], ["esink"])
    B.act(C["esink"][:], C["esink"][:], AF.Exp, ["esink"], ["esink"])
    gmix, kgm = load_gain(B, C, W["mix_norm"][L:L + 1, :], 0)
    ggn, kgg = load_gain(B, C, W["group_norm"][L:L + 1, 0:1024], 1)
    B.v("dve", "memset", [], ["vE"], vE[:, :, :, 64:65], 1.0)

    for t in range(16):
        xt = C["xt"][t % 2]
        kx = "xt%d" % (t % 2)
        B.dma("sp", xt[:], src[t * 128:(t + 1) * 128, :], [ksrc[t // 4]], [kx])
        norm_T(B, C, xt[:], kx, 2048, gmix[:, 0:2048], kgm, hT[:, :, t * 128:(t + 1) * 128], "hT", t % 2)

    if stop <= 0:
        return
    slabs = [A.view(WS + i * 16384, [16, 512], BF16) for i in range(2)]
    stg = [A.view(WS + 32768 + i * 2048, [512], F32) for i in range(4)]
    stgb = [A.view(WS + 32768 + i * 2048, [512], BF16) for i in range(4)]
    wqk = W["w_in_qk"][L].rearrange("(kc p) f -> p kc f", p=128)
    wrg = W["w_in"][L].rearrange("(kc p) f -> p kc f", p=128)
    wtm = W["w_in_tm"][L].rearrange("(kc p) f -> p kc f", p=128)
    import os as _os
    for si in range(int(_os.environ.get('M2_SLABS', '10'))):
        n = B.alt("wsl", [0, 1])
        ksl = "wsl%d" % n
        if si < 6:
            B.dma("pool", slabs[n], wqk[:, :, si * 512:(si + 1) * 512], [], [ksl])
        else:
            c0 = 2072 + (si - 6) * 512
            B.dma("pool", slabs[n], wrg[:, :, c0:c0 + 512], [], [ksl])
        for j in range(4):
            ch = si * 4 + j if si < 6 else (si - 6) * 4 + j
            if si == 5 and j >= 2:
                continue
            for tc in range(4):
                pb = B.alt("pjbank", [0, 1, 2, 3, 4, 5])
                kb = "bk%d" % pb
                for kc in range(16):
                    B.mm(banks[pb][:], slabs[n][:, kc, j * 128:(j + 1) * 128], hT[:, kc, tc * 512:(tc + 1) * 512],
                         kc == 0, kc == 15, [ksl, "hT"], [kb])
                sn = B.alt("stg", [0, 1, 2, 3])
                kst = "stg%d" % sn
                dst_sb = stgb[sn] if si < 6 else stg[sn]
                if B.alt("pjcopy", ["act", "dve"]) == "act":
                    B.act(dst_sb, banks[pb][:], AF.Copy, [kb], [kst])
                else:
                    B.v("dve", "tensor_copy", [kb], [kst], out=dst_sb, in_=banks[pb][:])
                if si < 6:
                    B.dma("sp", qkT[ch * 128:(ch + 1) * 128, tc * 512:(tc + 1) * 512], dst_sb, [kst], ["qk%d" % ch])
                else:
                    B.dma("sp", rgT[ch * 128:(ch + 1) * 128, tc * 512:(tc + 1) * 512], dst_sb, [kst], ["rg%d" % ch])
    n = B.alt("wsl", [0, 1])
    ksl = "wsl%d" % n
    if not _os.environ.get("TM_SKIP_DMA"):
        B.dma("pool", slabs[n][:, :, 0:408], wtm, [], [ksl])
    for t in range(int(_os.environ.get('M2_TM', '16'))):
        pb = B.alt("pjbank", [0, 1, 2, 3, 4, 5])
        kb = "bk%d" % pb
        for kc in range(16):
            B.mm(banks[pb][:, 0:408], hT[:, kc, t * 128:(t + 1) * 128], slabs[n][:, kc, 0:408], kc == 0, kc == 15,
                 [ksl, "hT"], [kb])
        if not _os.environ.get("TM_SKIP_ACT"):
            B.act(vE[:, t, :, 0:64], banks[pb][:, 0:384].rearrange("p (s d) -> p s d", s=6), AF.Copy, [kb], ["vE"])
        if not _os.environ.get("TM_SKIP_GL"):
            B.v("dve", "tensor_copy", [kb], ["gl"], out=gl[:, t, :], in_=banks[pb][:, 384:408])

    if stop <= 1:
        return
    B.barrier()
    f32v = [A.view(WS + i * 8192, [2048], F32) for i in range(6)]
    xcb = A.view(WS + 49152, [2048], BF16)
    wab = A.view(WS + 53248, [2, 128], BF16)
    sqlo = A.view(WS + 53760, [2048], BF16)
    oneb = C["oneb"]
    ssqC = banks[7]
    rgv = rgT.rearrange("(c p) t -> p c t", p=128)
    PVOFF = {"cw0": 0, "cw1": 8, "cw2": 16, "cw3": 24, "cb": 32, "ba": 40, "bx": 48, "lam": 56, "gc": 64}
    PVC = lambda name, cc: pv_[:, PVOFF[name] + cc:PVOFF[name] + cc + 1]
    nsp = C["nsp8"]
    B.act(nsp[:], pv_[:, 56:64], AF.Exp, ["pvec"], ["nsp8"], scale=-1.0)
    B.act(nsp[:], nsp[:], AF.Ln, ["nsp8", "onec"], ["nsp8"], bias=C["one"][:], scale=1.0)
    B.v("dve", "tensor_scalar", ["nsp8"], ["nsp8"], out=nsp[:], in0=nsp[:], scalar1=-8.0, scalar2=None, op0=ALU.mult)
    for cc in range(8):
        xr, xg, xc, r_, i_, t_ = f32v
        kxr, kxg, kxc, kr_, ki_, kt_ = ["f32v%d" % i for i in range(6)]
        B.dma("sp", xr, rgv[:, cc, :], ["rg%d" % cc], [kxr])
        B.dma("sp", xg, rgv[:, 8 + cc, :], ["rg%d" % (8 + cc)], [kxg])
        B.dma("pool", wab[:, 0, :], W["wabd"][L, cc], [], ["wab"])
        B.dma("pool", wab[:, 1, :], W["wxbd"][L, cc], [], ["wab"])
        B.v("dve", "tensor_scalar", [kxr, "pvec"], [kxc], out=xc, in0=xr, scalar1=PVC("cw3", cc),
            scalar2=PVC("cb", cc), op0=ALU.mult, op1=ALU.add)
        for j in range(3):
            sh = 3 - j
            B.v("dve", "scalar_tensor_tensor", [kxr, kxc, "pvec"], [kxc], out=xc[:, sh:], in0=xr[:, 0:S - sh],
                scalar=PVC("cw%d" % j, cc), in1=xc[:, sh:], op0=ALU.mult, op1=ALU.add)
        B.act(xcb, xc, AF.Copy, [kxc], ["xcb"])
        for gi, (dstg, kdg, bname) in enumerate([(r_, kr_, "ba"), (i_, ki_, "bx")]):
            for tc in range(4):
                pb = B.alt("cgbank", [0, 1, 2, 3])
                kb = "bk%d" % pb
                B.mm(banks[pb][:], wab[:, gi, :], xcb[:, tc * 512:(tc + 1) * 512], True, True, ["wab", "xcb"], [kb])
                B.act(dstg[:, tc * 512:(tc + 1) * 512], banks[pb][:], AF.Sigmoid, [kb, "pvec"], [kdg],
                      bias=PVC(bname, cc), scale=1.0)
        B.act(r_, r_, AF.Exp, [kr_, "nsp8"], [kr_], scale=nsp[:, cc:cc + 1])
        B.v("dve", "tensor_tensor", [kr_], [kt_], out=t_, in0=r_, in1=r_, op=ALU.mult)
        B.v("dve", "tensor_scalar", [kt_], [kt_], out=t_, in0=t_, scalar1=-1.0, scalar2=1.0, op0=ALU.mult, op1=ALU.add)
        B.act(t_, t_, AF.Sqrt, [kt_], [kt_])
        B.v("dve", "tensor_tensor", [kt_, ki_], [kt_], out=t_, in0=t_, in1=i_, op=ALU.mult)
        B.v("dve", "tensor_tensor", [kt_, kxc], [kt_], out=t_, in0=t_, in1=xc, op=ALU.mult)
        B.v("dve", "tensor_tensor_scan", [kr_, kt_], [ki_], out=i_, data0=r_, data1=t_, initial=0.0,
            op0=ALU.mult, op1=ALU.add)
        gelu_tanh(B, None, xg, kxg, t_, kt_, xr, kxr)
        B.v("dve", "tensor_tensor", [kxr, ki_], [kxc], out=xc, in0=xr, in1=i_, op=ALU.mult)
        B.act(t_, xc, AF.Square, [kxc], [kt_])
        B.v("dve", "tensor_copy", [kt_], ["xcb"], out=xcb, in_=t_)
        B.v("dve", "tensor_tensor", [kt_, "xcb"], [kt_], out=t_, in0=t_, in1=xcb, op=ALU.subtract)
        B.v("dve", "tensor_copy", [kt_], ["sqlo"], out=sqlo, in_=t_)
        for t in range(16):
            col = cc * 16 + t
            B.mm(ssqC[:, col:col + 1], xcb[:, t * 128:(t + 1) * 128], oneb[:, 0:1], True, False,
                 ["xcb", "onec"], ["bk7"])
            B.mm(ssqC[:, col:col + 1], sqlo[:, t * 128:(t + 1) * 128], oneb[:, 0:1], False, True,
                 ["sqlo", "onec"], ["bk7"])
        B.v("dve", "tensor_scalar", [kxc, "pvec"], ["yT"], out=yT[:, 8 + cc, :], in0=xc, scalar1=PVC("gc", cc),
            scalar2=None, op0=ALU.mult)
    rsC = C["rstdC"]
    B.v("dve", "tensor_reduce", ["bk7"], ["rstdC"], out=rsC[:], in_=ssqC[:, 0:128].rearrange("p (c t) -> p t c", c=8),
        axis=AX.X, op=ALU.add)
    B.act(rsC[:], rsC[:], AF.Sqrt, ["rstdC", "epsc"], ["rstdC"], bias=C["eps"][:], scale=1.0 / 1024)
    B.v("dve", "reciprocal", ["rstdC"], ["rstdC"], out=rsC[:], in_=rsC[:])

    if stop <= 2:
        return
    B.barrier()
    qkv = qkT.rearrange("(c p) t -> p c t", p=128)
    qaT = A.view(WS, [4, 2048], BF16)
    kaT = A.view(WS + 16384, [4, 2048], BF16)
    C["pT"] = [A.view(WS + 49152 + i * 1024, [512], BF16) for i in range(3)]
    ytile = [A.view(WS + 53248 + i * 2048, [8, 64], F32) for i in range(2)]
    B.dma("sp", qaT, qkv[:, 0:4, :], ["qk0", "qk1", "qk2", "qk3"], ["qaT"])
    B.dma("sp", kaT, qkv[:, 4:8, :], ["qk4", "qk5", "qk6", "qk7"], ["kaT"])
    identb, diagb, edgeb = K["identb"], K["diagb"], K["edgeb"]
    for qt in range(16):
        yt_ = ytile[qt % 2]
        kyt = "ytile%d" % (qt % 2)
        for kv in range(2):
            blocks = []
            kts = [qt - 1, qt] if qt > 0 else [qt]
            for g in range(4):
                h = kv * 4 + g
                base = 64 * (h % 2)
                for kt in kts:
                    mb = diagb if kt == qt else edgeb
                    blocks.append(dict(
                        kT=kaT[:, kv * 2 + h % 2, kt * 128:(kt + 1) * 128],
                        qT=qaT[:, h // 2, qt * 128:(qt + 1) * 128],
                        extras=[(identb[:], mb[:], ["kconst"])],
                        v=vE[:, kt, kv, 0:65], oc=(g * 65, 65), rk=["qaT", "kaT", "vE"]))
            ob = B.alt("obank", [3, 4, 5])
            kob = "bk%d" % ob
            attn_run(B, C, blocks, banks[ob], kob)
            ov = banks[ob][:, 0:260].rearrange("p (g c) -> p g c", c=65)
            den = C["den"][0]
            B.v("dve", "tensor_tensor", [kob, "esink"], ["den0"], out=den[:], in0=ov[:, :, 64],
                in1=C["esink"][:, kv * 4:(kv + 1) * 4], op=ALU.add)
            B.v("dve", "reciprocal", ["den0"], ["den0"], out=den[:], in_=den[:])
            B.v("dve", "tensor_tensor", [kob, "den0"], [kyt], out=yt_[:, kv * 4:(kv + 1) * 4, :], in0=ov[:, :, 0:64],
                in1=den[:].unsqueeze(2).broadcast_to([128, 4, 64]), op=ALU.mult)
        norm_T(B, C, yt_.rearrange("p h d -> p (h d)"), kyt, 512, ggn[:, 0:512], kgg,
               yT[:, 0:4, qt * 128:(qt + 1) * 128], "yT", qt % 2)

    if stop <= 3:
        return
    B.barrier()
    qbT = A.view(WS, [4, 2048], BF16)
    ksT = A.view(WS + 16384, [4, 2048], BF16)
    kwT = A.view(WS + 32768, [4, 2048], BF16)
    w1sb = A.view(WS + 16384, [32, 256], BF16)
    cxT = A.view(WS + 32768, [2048], BF16)
    cxz = [A.view(WS + 36864 + i * 4096, [2048], BF16) for i in range(2)]
    ctmp = [A.view(WS + 45056 + i * 512, [128], F32) for i in range(3)]
    B.dma("sp", qbT, qkv[:, 8:12, :], ["qk8", "qk9", "qk10", "qk11"], ["qbT"])
    KcT, VcE, hTb, w2sb, posT = C["KcT"], C["VcE"], C["hTb"], C["w2sb"], C["posT"]
    B.v("dve", "memset", [], ["KcT"], KcT[:], 0.0)
    B.v("dve", "memset", [], ["VcE"], VcE[:], 0.0)
    B.v("dve", "memset", ["VcE"], ["VcE"], VcE[:, :, 64:65], 1.0)
    B.dma("pool", C["ovl"][:], K["overlap_d"], [], ["ovl"])
    for kv in range(2):
        B.v("dve", "tensor_copy", ["ovl", "VcE"], ["VcE"], out=VcE[:, kv, 65:97], in_=C["ovl"][:])
    for i in range(2):
        B.dma("sp", cxT, qkv[:, 20 + i, :], ["qk%d" % (20 + i)], ["cxT"])
        for kv in range(2):
            B.v("dve", "memset", [], ["cxz%d" % kv], cxz[kv], 0.0)
            B.v("dve", "tensor_copy", ["cxT"], ["cxz%d" % kv], out=cxz[kv][64 * kv:64 * kv + 64, :],
                in_=cxT[64 * kv:64 * kv + 64, :])
        w1v = W["w1r"][L, i].rearrange("p (a b) -> p a b", a=32)
        for q4 in range(4):
            B.dma("pool", w1sb[:, q4 * 8:(q4 + 1) * 8, :], w1v[:, q4 * 8:(q4 + 1) * 8, :], [], ["w1sb"])
        B.dma("pool", w2sb[:].rearrange("p a b -> p (a b)"), W["w2r"][L, i], [], ["w2sb"])
        B.dma("pool", posT[:], W["posT"][L, i], [], ["posT"])
        if _os.environ.get("NSA_STOP") == "dma":
            continue
        hb = banks[0]
        first = True
        for kv in range(2):
            base = 64 * kv
            for jc in range(2):
                blk = kv * 2 + jc
                for l in range(32):
                    if _os.environ.get("NSA_MM") != "nomain":
                        B.mm(hb[:, blk * 128:blk * 128 + 127], w1sb[:, l, jc * 128:(jc + 1) * 128],
                             cxz[kv][:, l:l + 16 * 126 + 1:16], first, False, ["w1sb", "cxz%d" % kv], ["bk0"])
                        first = False
                    if _os.environ.get("NSA_MM") == "nopos":
                        continue
                    B.mm(hb[:, blk * 128 + 127:blk * 128 + 128], w1sb[:, l, jc * 128:(jc + 1) * 128],
                         posT[:, l:l + 1], False, (blk == 3 and l == 31), ["w1sb", "posT"], ["bk0"])
        if _os.environ.get("NSA_STOP") == "h":
            continue
        for blk in range(4):
            jc = blk % 2
            cb = C["cb"]
            B.v("dve", "tensor_tensor", ["bk0", "pvec"], ["cb"], out=cb[:], in0=hb[:, blk * 128 + 127:blk * 128 + 128],
                in1=pv_[:, 72 + 2 * i + jc:72 + 2 * i + jc + 1], op=ALU.add)
            B.v("dve", "tensor_scalar", ["bk0", "cb"], ["ctmp0"], out=ctmp[0][:, 0:127], in0=hb[:, blk * 128:blk * 128 + 127],
                scalar1=cb[:], scalar2=None, op0=ALU.add)
            gelu_tanh(B, None, ctmp[0][:, 0:127], "ctmp0", ctmp[1][:, 0:127], "ctmp1", hTb[:, blk, 0:127], "hTb")
        if _os.environ.get("NSA_STOP") == "g":
            continue
        ob = banks[1]
        if i == 0:
            for kv in range(2):
                for jc in range(2):
                    B.mm(ob[:, kv * 128:kv * 128 + 127], w2sb[:, jc, :], hTb[:, kv * 2 + jc, 0:127],
                         kv == 0 and jc == 0, kv == 1 and jc == 1, ["w2sb", "hTb"], ["bk1"])
            for kv in range(2):
                for par in range(2):
                    B.v("dve", "tensor_copy", ["bk1"], ["KcT"], out=KcT[64 * par:64 * par + 64, kv * 2 + par, 0:127],
                        in_=ob[64 * par:64 * par + 64, kv * 128:kv * 128 + 127])
        else:
            for kv in range(2):
                for jc in range(2):
                    B.mm(ob[0:127, kv * 64:(kv + 1) * 64], hTb[:, kv * 2 + jc, 0:127], w2sb[:, jc, 0:64],
                         kv == 0 and jc == 0, kv == 1 and jc == 1, ["w2sb", "hTb"], ["bk1"])
            B.v("dve", "tensor_copy", ["bk1"], ["VcE"], out=VcE[0:127, :, 0:64],
                in_=ob[0:127, 0:128].rearrange("p (k d) -> p k d", k=2))
    if _os.environ.get("NSA_STOP") in ("cmp", "dma", "h", "g"):
        return
    B.barrier()
    B.dma("sp", ksT, qkv[:, 12:16, :], ["qk12", "qk13", "qk14", "qk15"], ["ksT"])
    B.dma("sp", kwT, qkv[:, 16:20, :], ["qk16", "qk17", "qk18", "qk19"], ["kwT"])
    cmaskb, Eexp = K["cmaskb"], K["Eexp"]
    selA, selB = K["selA"], K["selB"]
    sgt, imp, imp2, imp3, m8, m8b, negb, negT = (C[k] for k in
                                                  ["sgt", "imp", "imp2", "imp3", "m8", "m8b", "negb", "negT"])
    tmpM = C["tmpM"]
    for qt in range(16):
        yt_ = ytile[qt % 2]
        kyt = "ytile%d" % (qt % 2)
        B.act(sgt[:], gl[:, qt, :], AF.Sigmoid, ["gl"], ["sgt"])
        for kv in range(2):
            blocks = []
            for g in range(4):
                h = kv * 4 + g
                base = 64 * (h % 2)
                blocks.append(dict(
                    kT=KcT[:, kv * 2 + h % 2, :],
                    qT=qbT[:, h // 2, qt * 128:(qt + 1) * 128],
                    extras=[(identb[:], cmaskb[:, qt * 128:(qt + 1) * 128], ["kconst"])],
                    v=VcE[:, kv, 0:97], oc=(g * 97, 97), rk=["qbT", "KcT", "VcE"]))
            attn_run(B, C, blocks, banks[3], "bk3")
            ovc = banks[3][:, 0:388].rearrange("p (g c) -> p g c", c=97)
            dc, ds, dw = C["den"][0], C["den"][1], C["den"][2]
            B.v("dve", "tensor_scalar", ["bk3"], ["den0"], out=dc[:], in0=ovc[:, :, 64], scalar1=1e-30, scalar2=None,
                op0=ALU.add)
            B.v("dve", "reciprocal", ["den0"], ["den0"], out=dc[:], in_=dc[:])
            B.v("dve", "tensor_tensor", ["bk3", "den0"], ["tmpM"], out=tmpM[:], in0=ovc[:, :, 65:97],
                in1=dc[:].unsqueeze(2).broadcast_to([128, 4, 32]), op=ALU.mult)
            B.v("dve", "tensor_reduce", ["tmpM"], ["imp"], out=imp[:], in_=tmpM[:].rearrange("p g j -> p j g"),
                axis=AX.X, op=ALU.add)
            B.v("dve", "tensor_tensor", ["imp", "kconst"], ["imp2"], out=imp2[:], in0=imp[:], in1=selA[:, qt, :], op=ALU.mult)
            B.v("dve", "tensor_tensor", ["imp2", "kconst"], ["imp2"], out=imp2[:], in0=imp2[:], in1=selB[:, qt, :], op=ALU.add)
            B.v("dve", "max", ["imp2"], ["m8"], out=m8[:], in_=imp2[:])
            B.v("dve", "match_replace", ["imp2", "m8"], ["imp3"], out=imp3[:], in_to_replace=m8[:], in_values=imp2[:],
                imm_value=-3.0e38)
            B.v("dve", "max", ["imp3"], ["m8b"], out=m8b[:], in_=imp3[:])
            B.v("dve", "tensor_scalar", ["imp2", "m8b"], ["negb"], out=negb[:], in0=imp2[:], scalar1=m8b[:, 7:8],
                scalar2=None, op0=ALU.is_ge)
            B.v("dve", "tensor_scalar", ["negb"], ["negb"], out=negb[:], in0=negb[:], scalar1=-NEG, scalar2=NEG,
                op0=ALU.mult, op1=ALU.add)
            B.tr(banks[6][0:32, 0:128], negb[:], C["identf"][:], ["negb", "ident"], ["bk6"])
            B.v("dve", "tensor_copy", ["bk6"], ["negT"], out=negT[0:32, :], in_=banks[6][0:32, 0:128])
            blocks = []
            for g in range(4):
                h = kv * 4 + g
                base = 64 * (h % 2)
                for kt in range(qt + 1):
                    ex = [(Eexp[:, kt, :], negT[:], ["kconst", "negT"])]
                    if kt == qt:
                        ex.append((identb[:], diagb[:], ["kconst"]))
                    blocks.append(dict(
                        kT=ksT[:, kv * 2 + h % 2, kt * 128:(kt + 1) * 128],
                        qT=qbT[:, h // 2, qt * 128:(qt + 1) * 128],
                        extras=ex, v=vE[:, kt, 2 + kv, 0:65], oc=(g * 65, 65), rk=["qbT", "ksT", "vE"]))
            attn_run(B, C, blocks, banks[4], "bk4")
            blocks = []
            for g in range(4):
                h = kv * 4 + g
                base = 64 * (h % 2)
                for kt in range(max(0, qt - 4), qt + 1):
                    ex = []
                    if kt == qt:
                        ex.append((identb[:], diagb[:], ["kconst"]))
                    if kt == qt - 4:
                        ex.append((identb[:], edgeb[:], ["kconst"]))
                    blocks.append(dict(
                        kT=kwT[:, kv * 2 + h % 2, kt * 128:(kt + 1) * 128],
                        qT=qbT[:, h // 2, qt * 128:(qt + 1) * 128],
                        extras=ex, v=vE[:, kt, 4 + kv, 0:65], oc=(g * 65, 65), rk=["qbT", "kwT", "vE"]))
            attn_run(B, C, blocks, banks[5], "bk5")
            ovs = banks[4][:, 0:260].rearrange("p (g c) -> p g c", c=65)
            ovw = banks[5][:, 0:260].rearrange("p (g c) -> p g c", c=65)
            B.v("dve", "reciprocal", ["bk4"], ["den1"], out=ds[:], in_=ovs[:, :, 64])
            B.v("dve", "reciprocal", ["bk5"], ["den2"], out=dw[:], in_=ovw[:, :, 64])
            B.v("dve", "tensor_tensor", ["den0", "sgt"], ["den0"], out=dc[:], in0=dc[:], in1=sgt[:, kv * 4:kv * 4 + 4], op=ALU.mult)
            B.v("dve", "tensor_tensor", ["den1", "sgt"], ["den1"], out=ds[:], in0=ds[:], in1=sgt[:, 8 + kv * 4:8 + kv * 4 + 4], op=ALU.mult)
            B.v("dve", "tensor_tensor", ["den2", "sgt"], ["den2"], out=dw[:], in0=dw[:], in1=sgt[:, 16 + kv * 4:16 + kv * 4 + 4], op=ALU.mult)
            ysl = yt_[:, kv * 4:(kv + 1) * 4, :]
            ot = C["otmp"]
            B.v("dve", "tensor_tensor", ["bk3", "den0"], [kyt], out=ysl, in0=ovc[:, :, 0:64],
                in1=dc[:].unsqueeze(2).broadcast_to([128, 4, 64]), op=ALU.mult)
            B.v("dve", "tensor_tensor", ["bk4", "den1"], ["otmp"], out=ot[:], in0=ovs[:, :, 0:64],
                in1=ds[:].unsqueeze(2).broadcast_to([128, 4, 64]), op=ALU.mult)
            B.v("dve", "tensor_tensor", ["otmp", kyt], [kyt], out=ysl, in0=ysl, in1=ot[:], op=ALU.add)
            B.v("dve", "tensor_tensor", ["bk5", "den2"], ["otmp"], out=ot[:], in0=ovw[:, :, 0:64],
                in1=dw[:].unsqueeze(2).broadcast_to([128, 4, 64]), op=ALU.mult)
            B.v("dve", "tensor_tensor", ["otmp", kyt], [kyt], out=ysl, in0=ysl, in1=ot[:], op=ALU.add)
        norm_T(B, C, yt_.rearrange("p h d -> p (h d)"), kyt, 512, ggn[:, 512:1024], kgg,
               yT[:, 4:8, qt * 128:(qt + 1) * 128], "yT", qt % 2)

    if dbg is not None:
        B.dma("sp", dbg.rearrange("(c p) t -> p c t", p=128), yT, ["yT"], ["dbg"])
    if stop <= 4:
        return
    B.barrier()
    wos = [A.view(WS + i * 16384, [16, 512], BF16) for i in range(2)]
    xss = [A.view(WS + 32768 + i * 2048, [512], F32) for i in range(4)]
    oss = [A.view(WS + 40960 + i * 2048, [512], F32) for i in range(4)]
    wov = W["w_out"][L].rearrange("(kc p) d -> p kc d", p=128)
    for s in range(4):
        n = B.alt("wos", [0, 1])
        kwo = "wos%d" % n
        B.dma("pool", wos[n], wov[:, :, s * 512:(s + 1) * 512], [], [kwo])
        for tt in range(16):
            pa = B.alt("wobank", [0, 2, 4])
            pc = pa + 1
            for kc in range(8):
                B.mm(banks[pa][:], yT[:, kc, tt * 128:(tt + 1) * 128], wos[n][:, kc, :], kc == 0, kc == 7,
                     ["yT", kwo], ["bk%d" % pa])
            for kc in range(8, 16):
                B.mm(banks[pc][:], yT[:, kc, tt * 128:(tt + 1) * 128], wos[n][:, kc, :], kc == 8, kc == 15,
                     ["yT", kwo], ["bk%d" % pc])
            m = B.alt("xso", [0, 1, 2, 3])
            kxs, kos = "mxs%d" % m, "mos%d" % m
            B.dma("sp", xss[m], src[tt * 128:(tt + 1) * 128, s * 512:(s + 1) * 512], [ksrc[tt // 4]], [kxs])
            B.v("dve", "tensor_tensor", ["bk%d" % pa, kxs], [kxs], out=xss[m], in0=banks[pa][:], in1=xss[m], op=ALU.add)
            B.v("dve", "scalar_tensor_tensor", ["bk%d" % pc, kxs, "rstdC"], [kos], out=oss[m], in0=banks[pc][:],
                scalar=rsC[:, tt:tt + 1], in1=xss[m], op0=ALU.mult, op1=ALU.add)
            B.dma("sp", dst[tt * 128:(tt + 1) * 128, s * 512:(s + 1) * 512], oss[m], [kos], [kdst[tt // 4]])


def _dup(a, b):
    return list(range(a, b)) * 2


def _top(a, b):
    return list(range(a, b)) + [-1] * 64


def _bot(a, b):
    return [-1] * 64 + list(range(a, b))


QK_COLS = (list(range(0, 512)) + _top(512, 576) + _bot(512, 576) + _top(576, 640) + _bot(576, 640) +
           list(range(768, 1280)) + _top(1536, 1600) + _bot(1536, 1600) + _top(1600, 1664) + _bot(1600, 1664) +
           _top(1792, 1856) + _bot(1792, 1856) + _top(1856, 1920) + _bot(1856, 1920) +
           list(range(1280, 1408)) + list(range(1408, 1536)) + [-1] * 256)
NQK = len(QK_COLS)
TM_COLS = list(range(640, 768)) + list(range(1664, 1792)) + list(range(1920, 2048)) + list(range(2048, 2072))


def host_consts():
    k = np.arange(128)[:, None]
    q = np.arange(128)[None, :]
    c = {}
    c["identf_d"] = np.eye(128, dtype=np.float32)
    c["diag_d"] = np.where(k <= q, 0.0, NEG).astype(np.float32)
    c["edge_d"] = np.where(k > q, 0.0, NEG).astype(np.float32)
    cc = np.arange(128)[:, None]
    qq = np.arange(S)[None, :]
    c["cmask_d"] = np.where((cc < 127) & (16 * cc + 31 <= qq), 0.0, NEG).astype(np.float32)
    E = np.zeros((128, 16, 128), np.float32)
    for kt in range(16):
        for kk in range(128):
            E[2 * kt + kk // 64, kt, kk] = 1.0
    c["eexp_d"] = E.reshape(128, 2048)
    qpos = np.arange(S)[:, None]
    j = np.arange(32)[None, :]
    cur = qpos // 64
    forced = (j == 0) | (j == cur) | (j == cur - 1)
    valid = j * 64 <= qpos
    selA = np.where(forced | ~valid, 0.0, 1.0).astype(np.float32)
    selB = np.where(~valid, -1e30, np.where(forced, 1e4, 0.0)).astype(np.float32)
    c["selA_d"] = np.ascontiguousarray(selA.reshape(16, 128, 32).transpose(1, 0, 2)).reshape(128, 512)
    c["selB_d"] = np.ascontiguousarray(selB.reshape(16, 128, 32).transpose(1, 0, 2)).reshape(128, 512)
    cs = np.arange(127)[:, None] * 16
    ss = np.arange(32)[None, :] * 64
    ov = np.clip(np.minimum(cs + 32, ss + 64) - np.maximum(cs, ss), 0, None) / 32.0
    o = np.zeros((128, 32), np.float32)
    o[:127] = ov
    c["overlap_d"] = o
    return c


def host_layout(inp):
    f = lambda a: np.ascontiguousarray(np.asarray(a, dtype=np.float32))
    d = {}
    for kk in ["ffn1_w_gate", "ffn1_w_up", "ffn1_w_down", "ffn2_w_gate", "ffn2_w_up", "ffn2_w_down", "w_in",
               "w_out", "ffn1_norm", "ffn2_norm", "mix_norm", "group_norm"]:
        d[kk] = f(inp[kk])
    d["final_norm"] = f(inp["final_norm"]).reshape(1, D)
    w_in = d["w_in"]
    cols = np.asarray(QK_COLS)
    wqk = np.zeros((DEPTH, D, NQK), np.float32)
    wqk[:, :, cols >= 0] = w_in[:, :, cols[cols >= 0]]
    d["w_in_qk"] = wqk
    d["w_in_tm"] = f(w_in[:, :, TM_COLS])
    pvec = np.zeros((DEPTH, 128, NPV), np.float32)
    cw, cb = f(inp["conv_w"]), f(inp["conv_b"])
    for l in range(DEPTH):
        for j in range(4):
            pvec[l, :, j * 8:(j + 1) * 8] = cw[l, j].reshape(8, 128).T
        pvec[l, :, 32:40] = cb[l].reshape(8, 128).T
        pvec[l, :, 40:48] = f(inp["lru_ba"])[l].reshape(8, 128).T
        pvec[l, :, 48:56] = f(inp["lru_bx"])[l].reshape(8, 128).T
        pvec[l, :, 56:64] = f(inp["lru_lambda"])[l].reshape(8, 128).T
        pvec[l, :, 64:72] = d["group_norm"][l, 1024:].reshape(8, 128).T
        for i in range(2):
            pvec[l, :, 72 + 2 * i:74 + 2 * i] = f(inp["cmp_b1"])[l, i].reshape(2, 128).T
    d["pvec"] = pvec
    d["sinks"] = f(np.broadcast_to(f(inp["swa_sinks"])[:, None, :], (DEPTH, 128, 8)))
    pos = f(inp["cmp_pos"])
    posT = np.zeros((DEPTH, 2, 128, 32), np.float32)
    posT[:, :, 0:64, :] = pos.transpose(0, 1, 3, 2)
    d["posT"] = posT
    w1 = f(inp["cmp_w1"]).reshape(DEPTH, 2, 32, 64, 256).transpose(0, 1, 3, 2, 4)
    d["w1r"] = f(np.tile(w1, (1, 1, 2, 1, 1))).reshape(DEPTH, 2, 128, 32 * 256)
    w2 = f(inp["cmp_w2"]).reshape(DEPTH, 2, 2, 128, 64).transpose(0, 1, 3, 2, 4)
    d["w2r"] = f(np.tile(w2, (1, 1, 1, 1, 2))).reshape(DEPTH, 2, 128, 256)
    for nm, src in [("wabd", "lru_wa"), ("wxbd", "lru_wx")]:
        w = f(inp[src])
        bd = np.zeros((DEPTH, 8, 128, 128), np.float32)
        for cc in range(8):
            bd[:, cc, 0:64, 0:64] = w[:, 2 * cc]
            bd[:, cc, 64:128, 64:128] = w[:, 2 * cc + 1]
        d[nm] = bd
    d.update(host_consts())
    return d


IN_SHAPES = {
    "ffn1_w_gate": [DEPTH, D, DFF], "ffn1_w_up": [DEPTH, D, DFF], "ffn1_w_down": [DEPTH, DFF, D],
    "ffn2_w_gate": [DEPTH, D, DFF], "ffn2_w_up": [DEPTH, D, DFF], "ffn2_w_down": [DEPTH, DFF, D],
    "w_in": [DEPTH, D, DIN], "w_out": [DEPTH, D, D], "ffn1_norm": [DEPTH, D], "ffn2_norm": [DEPTH, D],
    "mix_norm": [DEPTH, D], "group_norm": [DEPTH, D], "final_norm": [1, D],
    "w_in_qk": [DEPTH, D, NQK], "w_in_tm": [DEPTH, D, 408], "pvec": [DEPTH, 128, NPV], "sinks": [DEPTH, 128, 8],
    "posT": [DEPTH, 2, 128, 32], "w1r": [DEPTH, 2, 128, 8192], "w2r": [DEPTH, 2, 128, 256],
    "wabd": [DEPTH, 8, 128, 128], "wxbd": [DEPTH, 8, 128, 128],
    "identf_d": [128, 128], "diag_d": [128, 128], "edge_d": [128, 128], "cmask_d": [128, 2048],
    "eexp_d": [128, 2048], "selA_d": [128, 512], "selB_d": [128, 512], "overlap_d": [128, 32],
}


def setup_common(B, W):
    C = {}
    C["identf"] = B.sb("identf", [128, 128], F32)
    C["eps"] = B.sb("eps", [128, 1], F32)
    C["one"] = B.sb("one", [128, 1], F32)
    C["onef"] = B.sb("onef", [128, 2], F32)
    C["oneb"] = B.sb("oneb", [128, 2], BF16)
    C["junk"] = B.sb("junk", [128, 2048], BF16)
    C["ssq"] = [B.sb("ssq%d" % i, [128, 1], F32) for i in range(2)]
    C["rstd"] = [B.sb("rstd%d" % i, [128, 1], F32) for i in range(2)]
    C["xn"] = [B.sb("xn%d" % i, [128, 2048], F32) for i in range(2)]
    C["xt"] = [B.sb("xt%d" % i, [128, 2048], F32) for i in range(2)]
    C["gbc"] = [B.sb("gbc0", [128, 2048], F32), B.sb("gbc1", [128, 1024], F32)]
    C["pvec"] = B.sb("pvec", [128, NPV], F32)
    C["esink"] = B.sb("esink", [128, 8], F32)
    C["nsp8"] = B.sb("nsp8", [128, 8], F32)
    C["rstdC"] = B.sb("rstdC", [128, 16], F32)
    C["den"] = [B.sb("den%d" % i, [128, 4], F32) for i in range(3)]
    C["KcT"] = B.sb("KcT", [128, 4, 128], BF16)
    C["VcE"] = B.sb("VcE", [128, 2, 98], BF16)
    C["hTb"] = B.sb("hTb", [128, 4, 128], BF16)
    C["w2sb"] = B.sb("w2sb", [128, 2, 128], BF16)
    C["posT"] = B.sb("posT", [128, 32], BF16)
    C["ovl"] = B.sb("ovl", [128, 32], BF16)
    C["cb"] = B.sb("cbias", [128, 1], F32)
    for nm, shp in [("sgt", [128, 24]), ("imp", [128, 32]), ("imp2", [128, 32]), ("imp3", [128, 32]),
                    ("m8", [128, 8]), ("m8b", [128, 8]), ("negb", [128, 32]), ("tmpM", [128, 4, 32]),
                    ("otmp", [128, 4, 64])]:
        C[nm] = B.sb(nm, shp, F32)
    C["negT"] = B.sb("negT", [128, 128], BF16)
    K = {}
    K["identb"] = B.sb("identb", [128, 128], BF16)
    K["diagb"] = B.sb("diagb", [128, 128], BF16)
    K["edgeb"] = B.sb("edgeb", [128, 128], BF16)
    K["cmaskb"] = B.sb("cmaskb", [128, 2048], BF16)
    K["Eexp"] = B.sb("Eexp", [128, 16, 128], BF16)
    K["selA"] = B.sb("selA", [128, 16, 32], F32)
    K["selB"] = B.sb("selB", [128, 16, 32], F32)
    K["overlap_d"] = W["overlap_d"]
    C["arena"] = Arena(B)
    C["banks"] = [B.ps("bank%d" % i, [128, 512], F32) for i in range(8)]
    B.dma("sp", C["identf"][:], W["identf_d"], [], ["ident"])
    B.dma("pool", K["identb"][:], W["identf_d"], [], ["kconst"])
    B.dma("pool", K["diagb"][:], W["diag_d"], [], ["kconst"])
    B.dma("pool", K["edgeb"][:], W["edge_d"], [], ["kconst"])
    B.dma("pool", K["cmaskb"][:], W["cmask_d"], [], ["kconst"])
    B.dma("pool", K["Eexp"][:].rearrange("p a b -> p (a b)"), W["eexp_d"], [], ["kconst"])
    B.dma("sp", K["selA"][:].rearrange("p a b -> p (a b)"), W["selA_d"], [], ["kconst"])
    B.dma("sp", K["selB"][:].rearrange("p a b -> p (a b)"), W["selB_d"], [], ["kconst"])
    B.v("dve", "memset", [], ["epsc"], C["eps"][:], RMS_EPS)
    B.v("dve", "memset", [], ["negT"], C["negT"][:], 0.0)
    B.v("dve", "memset", [], ["onec"], C["one"][:], 1.0)
    B.v("dve", "memset", [], ["onec"], C["onef"][:], 1.0)
    B.v("dve", "memset", [], ["onec"], C["oneb"][:], 1.0)
    return C, K


def final_norm(B, C, src, ksrc, dst, kdst, g_bc, kg):
    for t in range(16):
        xt = C["xt"][t % 2]
        kx = "xt%d" % (t % 2)
        B.dma("sp", xt[:], src[t * 128:(t + 1) * 128, :], [ksrc[t // 4]], [kx])
        xn, kn = norm_T(B, C, xt[:], kx, 2048, g_bc[:, 0:2048], kg, None, None, t % 2, transpose=False)
        B.dma("sp", dst[t * 128:(t + 1) * 128, :], xn[:], [kn], [kdst])


def build_kernel():
    B = Builder()
    nc = B.nc
    W = {}
    for nm, shp in IN_SHAPES.items():
        W[nm] = nc.dram_tensor(nm, shp, F32, kind="ExternalInput").ap()
    x = nc.dram_tensor("x", [S, D], F32, kind="ExternalInput").ap()
    out = nc.dram_tensor("out", [S, D], F32, kind="ExternalOutput").ap()
    xa = B.dram("xa", [S, D], F32)
    xb = B.dram("xb", [S, D], F32)
    SCR = {"qkT": B.dram("qkT", [NQK, 2048], BF16), "rgT": B.dram("rgT", [2048, 2048], F32)}
    C, K = setup_common(B, W)
    cur, kcur = x, ["x%d" % c for c in range(4)]
    pp = [(xa, "xa"), (xb, "xb")]
    nxt = 0

    def ffn_block(pre, L, cur, kcur, dst, kd):
        g, kg = load_gain(B, C, W[pre + "_norm"][L:L + 1, :], 0)
        for c in range(4):
            ffn_chunk(B, C, cur[c * 512:(c + 1) * 512, :], dst[c * 512:(c + 1) * 512, :], kcur[c], kd[c], g, kg,
                      W[pre + "_w_gate"][L], W[pre + "_w_up"][L], W[pre + "_w_down"][L], "%s_%d" % (pre, L))
        B.barrier()

    for L in range(DEPTH):
        dst, kd = pp[nxt][0], ["%s_%d_%d_%d" % (pp[nxt][1], L, 0, c) for c in range(4)]
        ffn_block("ffn1", L, cur, kcur, dst, kd)
        cur, kcur, nxt = dst, kd, 1 - nxt
        dst, kd = pp[nxt][0], ["%s_%d_%d_%d" % (pp[nxt][1], L, 1, c) for c in range(4)]
        mixer(B, C, L, cur, dst, kcur, kd, W, K, SCR)
        B.barrier()
        cur, kcur, nxt = dst, kd, 1 - nxt
        dst, kd = pp[nxt][0], ["%s_%d_%d_%d" % (pp[nxt][1], L, 2, c) for c in range(4)]
        ffn_block("ffn2", L, cur, kcur, dst, kd)
        cur, kcur, nxt = dst, kd, 1 - nxt
    g, kg = load_gain(B, C, W["final_norm"], 0)
    final_norm(B, C, cur, kcur, out, "out", g, kg)
    B.S.add("sp", None, ["out"], [])
    B.S.emit()
    return B


_CACHE = {}


def kernel(**inputs):
    x = np.ascontiguousarray(np.asarray(inputs["x"], dtype=np.float32))
    shared = host_layout(inputs)
    if "B" not in _CACHE:
        _CACHE["B"] = build_kernel()
    B = _CACHE["B"]
    in_maps = []
    for c in range(8):
        m = {nm: shared[nm] for nm in IN_SHAPES}
        m["x"] = x[c]
        in_maps.append(m)
    res = run_bass_kernel_spmd(B.nc, in_maps, core_ids=list(range(8)))
    return np.stack([np.asarray(r["out"], dtype=np.float32).reshape(S, D) for r in res.results], axis=0)
```

```python
import numpy as np
from contextlib import ExitStack
import concourse.bass as bass
import concourse.mybir as mybir
from concourse.bass_utils import run_bass_kernel_spmd

F32 = mybir.dt.float32
BF16 = mybir.dt.bfloat16
I32 = mybir.dt.int32
AF = mybir.ActivationFunctionType
ALU = mybir.AluOpType
AX = mybir.AxisListType

D = 2048
S = 2048
DEPTH = 2
DFF = 5632
NFF = DFF // 128
DIN = 4120
RMS_EPS = 1e-6
SEM_CAP = 30000


class _Op:
    __slots__ = ("eng", "fn", "r", "w", "dma", "deps", "awaited", "sem", "semval")

    def __init__(self, eng, fn, r, w, dma):
        self.eng = eng
        self.fn = fn
        self.r = tuple(r)
        self.w = tuple(w)
        self.dma = dma
        self.deps = None
        self.awaited = False
        self.sem = None
        self.semval = 0


class Sched:
    def __init__(self, nc, es, n_dma_sems=20):
        self.nc = nc
        self.es = es
        self.ops = []
        self.ENG = {"pe": nc.tensor, "act": nc.scalar, "dve": nc.vector,
                    "pool": nc.gpsimd, "sp": nc.sync}
        self.n_dma_sems = n_dma_sems
        self._semh = {}
        self.cur_barrier = None

    def add(self, eng, fn, r=(), w=(), dma=False):
        w = list(w) + [k for k in r if k.startswith("bk")]
        r = [k for k in r if not k.startswith("bk")]
        op = _Op(eng, fn, r, w, dma)
        op.deps = set()
        if self.cur_barrier is not None:
            op.deps.add(self.cur_barrier)
        self.ops.append(op)

    def barrier(self, fn):
        op = _Op("dve", fn, (), (), False)
        op.deps = set()
        start = self.cur_barrier if self.cur_barrier is not None else 0
        last = {}
        for i in range(start, len(self.ops)):
            o = self.ops[i]
            if o.dma:
                op.deps.add(i)
            else:
                last[o.eng] = i
        op.deps.update(last.values())
        self.ops.append(op)
        self.cur_barrier = len(self.ops) - 1

    def _sem(self, name):
        if name not in self._semh:
            self._semh[name] = self.es.enter_context(self.nc.semaphore(name))
        return self._semh[name]

    def emit(self):
        ops = self.ops
        last_w, rd_eng, rd_dma = {}, {}, {}
        for i, op in enumerate(ops):
            deps = op.deps
            for k in op.r:
                if k in last_w:
                    deps.add(last_w[k])
            for k in op.w:
                if k in last_w:
                    deps.add(last_w[k])
                deps.update(rd_eng.get(k, {}).values())
                deps.update(rd_dma.get(k, ()))
            deps.discard(i)
            op.deps = deps
            for k in op.r:
                if op.dma:
                    rd_dma.setdefault(k, []).append(i)
                else:
                    rd_eng.setdefault(k, {})[op.eng] = i
            for k in op.w:
                last_w[k] = i
                rd_eng[k] = {}
                rd_dma[k] = []
        rr, last_user, dcount = {}, {}, {}
        for i, op in enumerate(ops):
            if op.dma:
                n = rr.get(op.eng, 0)
                rr[op.eng] = n + 1
                s = "d_%s_%d" % (op.eng, n % self.n_dma_sems)
                if s in last_user:
                    op.deps.add(last_user[s])
                last_user[s] = i
                dcount[s] = dcount.get(s, 0) + 1
                op.sem = s
                op.semval = 16 * dcount[s]
        for op in ops:
            for j in op.deps:
                d = ops[j]
                if d.dma:
                    continue
                if d.eng == op.eng and op.eng == "pe" and not op.dma:
                    continue
                d.awaited = True
        cnt = {}
        for op in ops:
            if not op.dma and op.awaited:
                c = cnt.get(op.eng, 0)
                cnt[op.eng] = c + 1
                op.sem = "e_%s_%d" % (op.eng, c // SEM_CAP)
                op.semval = c % SEM_CAP + 1
        waited = {e: {} for e in self.ENG}
        nwait = 0
        for op in ops:
            e = op.eng
            need = {}
            for j in op.deps:
                d = ops[j]
                if (not d.dma) and d.eng == e and e == "pe" and not op.dma:
                    continue
                if need.get(d.sem, 0) < d.semval:
                    need[d.sem] = d.semval
            for s, v in need.items():
                if waited[e].get(s, 0) < v:
                    self.ENG[e].wait_ge(self._sem(s), v)
                    waited[e][s] = v
                    nwait += 1
            if op.fn is None:
                continue
            ins = op.fn()
            if op.dma:
                ins.then_inc(self._sem(op.sem), 16)
            elif op.awaited:
                ins.then_inc(self._sem(op.sem), 1)
        self.stats = dict(n_ops=len(ops), n_wait=nwait, awaited=dict(cnt))


class Builder:
    def __init__(self):
        self.nc = bass.Bass("TRN2", target_bir_lowering=False)
        self.es = ExitStack()
        self.S = Sched(self.nc, self.es)
        self._n = 0
        self.rr = {}
        self.bar_tile = self.sb("bar_tile", [128, 1], F32)

    def sb(self, name, shape, dt):
        return self.es.enter_context(self.nc.sbuf_tensor("s_" + name, list(shape), dt))

    def ps(self, name, shape, dt):
        return self.es.enter_context(self.nc.psum_tensor("p_" + name, list(shape), dt))

    def dram(self, name, shape, dt, kind="Internal"):
        return self.nc.dram_tensor("d_" + name, list(shape), dt, kind=kind).ap()

    def dma(self, q, out, in_, r, w, **kw):
        eng = self.S.ENG[q]
        self.S.add(q, lambda: eng.dma_start(out=out, in_=in_, **kw), r, w, dma=True)

    def mm(self, out, lhsT, rhs, start, stop, r, w):
        nc = self.nc
        self.S.add("pe", lambda: nc.tensor.matmul(out, lhsT, rhs, start=start, stop=stop), r, w)

    def tr(self, out, in_, ident, r, w):
        nc = self.nc
        self.S.add("pe", lambda: nc.tensor.transpose(out, in_, ident), r, w)

    def act(self, out, in_, func, r, w, bias=None, scale=None, accum_out=None):
        nc = self.nc
        kw = {}
        if bias is not None:
            kw["bias"] = bias
        if scale is not None:
            kw["scale"] = scale
        if accum_out is not None:
            kw["accum_out"] = accum_out
        self.S.add("act", lambda: nc.scalar.activation(out=out, in_=in_, func=func, **kw), r, w)

    def v(self, eng, name, r, w, *a, **kw):
        e = self.S.ENG[eng]
        self.S.add(eng, lambda: getattr(e, name)(*a, **kw), r, w)

    def barrier(self):
        nc = self.nc
        bt = self.bar_tile
        self.S.barrier(lambda: nc.vector.memset(bt[:], 0.0))

    def alt(self, key, choices):
        n = self.rr.get(key, 0)
        self.rr[key] = n + 1
        return choices[n % len(choices)]


import os as _os0
_NODMA = bool(_os0.environ.get('FFN_NODMA'))
NEG = -30000.0
ARENA_BYTES = 143360
SC_BANKS = [0, 1, 2]
NPV = 76


class Arena:
    def __init__(self, B):
        self.t = B.sb("arena", [128, ARENA_BYTES // 4], F32)

    def view(self, off, shape, dt, nparts=128):
        esz = 4 if dt == F32 else 2
        n = 1
        for d in shape:
            n *= d
        assert off % 4 == 0 and (n * esz) % 4 == 0 and off + n * esz <= ARENA_BYTES, (off, shape)
        v = self.t[0:nparts, off // 4:(off + n * esz) // 4]
        if dt != F32:
            v = v.bitcast(dt)
        if len(shape) == 2:
            v = v.rearrange("p (a b) -> p a b", a=shape[0])
        elif len(shape) == 3:
            v = v.rearrange("p (a b c) -> p a b c", a=shape[0], b=shape[1])
        return v


def norm_T(B, C, xt, kx, width, g_bc, kg, dstT, kdst, slot, transpose=True):
    junk, ssq, rstd, xn = C["junk"], C["ssq"][slot], C["rstd"][slot], C["xn"][slot]
    kj, ks, kr, kn = "junk", "ssq%d" % slot, "rstd%d" % slot, "xn%d" % slot
    B.act(junk[:, 0:width], xt, AF.Square, [kx], [kj, ks], accum_out=ssq[:])
    B.act(rstd[:], ssq[:], AF.Sqrt, [ks, "epsc"], [kr], bias=C["eps"][:], scale=1.0 / width)
    B.v("dve", "reciprocal", [kr], [kr], out=rstd[:], in_=rstd[:])
    B.v("dve", "scalar_tensor_tensor", [kx, kr, kg], [kn], out=xn[:, 0:width], in0=xt, scalar=rstd[:],
        in1=g_bc, op0=ALU.mult, op1=ALU.mult)
    if not transpose:
        return xn, kn
    for q4 in range(width // 512):
        pb = B.alt("trbank", [6, 7])
        pt = C["banks"][pb]
        kp = "bk%d" % pb
        for j in range(4):
            kc = q4 * 4 + j
            B.tr(pt[:, j * 128:(j + 1) * 128], xn[:, kc * 128:(kc + 1) * 128], C["identf"][:],
                 [kn, "ident"], [kp])
        eng = B.alt("trcopy", ["act", "dve"])
        src3 = pt[:].rearrange("p (j c) -> p j c", j=4)
        if eng == "act":
            B.act(dstT[:, q4 * 4:(q4 + 1) * 4, :], src3, AF.Copy, [kp], [kdst])
        else:
            B.v("dve", "tensor_copy", [kp], [kdst], out=dstT[:, q4 * 4:(q4 + 1) * 4, :], in_=src3)


def load_gain(B, C, gvec, slot):
    n = gvec.shape[1]
    B.dma("sp", C["gbc"][slot][:, 0:n], gvec.partition_broadcast(128), [], ["gbc%d" % slot])
    return C["gbc"][slot], "gbc%d" % slot


def ffn_chunk(B, C, src, dst, ksrc, kdst, g_bc, kg, wg, wu, wd, uid):
    A = C["arena"]
    xnT = A.view(0, [16, 512], BF16)
    act = A.view(16384, [NFF, 512], BF16)
    wgs = [A.view(61440 + i * 8192, [16, 256], BF16) for i in range(3)]
    wus = [A.view(86016 + i * 8192, [16, 256], BF16) for i in range(3)]
    wds = [A.view(110592 + i * 4096, [4, 512], BF16) for i in range(3)]
    sgs = [A.view(122880 + i * 2048, [512], F32) for i in range(2)]
    xss = [A.view(126976 + i * 2048, [512], F32) for i in range(4)]
    oss = [A.view(135168 + i * 2048, [512], F32) for i in range(4)]
    for t in range(4):
        xt = C["xt"][t % 2]
        kx = "xt%d" % (t % 2)
        B.dma("sp", xt[:], src[t * 128:(t + 1) * 128, :], [ksrc], [kx])
        norm_T(B, C, xt[:], kx, 2048, g_bc[:, 0:2048], kg, xnT[:, :, t * 128:(t + 1) * 128], "xnT", t % 2)
    wgv = wg.rearrange("(kc p) f -> p kc f", p=128)
    wuv = wu.rearrange("(kc p) f -> p kc f", p=128)
    for sl in range(NFF // 2):
        n = B.alt("wgu", [0, 1, 2])
        kwg, kwu = "wgs%d" % n, "wus%d" % n
        if not _NODMA:
            B.dma("pool", wgs[n], wgv[:, :, sl * 256:(sl + 1) * 256], ["W" + uid], [kwg])
            B.dma("pool", wus[n], wuv[:, :, sl * 256:(sl + 1) * 256], ["W" + uid], [kwu])
        for j in range(2):
            ff = sl * 2 + j
            pb = B.alt("gubank", [0, 1])
            gps, ups = C["banks"][2 * pb], C["banks"][2 * pb + 1]
            kgp, kup = "bk%d" % (2 * pb), "bk%d" % (2 * pb + 1)
            for kc in range(16):
                B.mm(gps[:], wgs[n][:, kc, j * 128:(j + 1) * 128], xnT[:, kc, :], kc == 0, kc == 15,
                     [kwg, "xnT"], [kgp])
            for kc in range(16):
                B.mm(ups[:], wus[n][:, kc, j * 128:(j + 1) * 128], xnT[:, kc, :], kc == 0, kc == 15,
                     [kwu, "xnT"], [kup])
            ksg = "sg%d" % pb
            B.act(sgs[pb], gps[:], AF.Silu, [kgp], [ksg])
            B.v("dve", "tensor_tensor", [ksg, kup], ["act%d" % ff], out=act[:, ff, :], in0=sgs[pb],
                in1=ups[:], op=ALU.mult)
    wdv = wd.rearrange("(f p) d -> p f d", p=128)
    for s in range(4):
        accs = C["banks"][4:8]
        for fg in range(NFF // 4):
            n = B.alt("wd", [0, 1, 2])
            kwd = "wds%d" % n
            if not _NODMA:
                B.dma("pool", wds[n], wdv[:, fg * 4:(fg + 1) * 4, s * 512:(s + 1) * 512], ["W" + uid], [kwd])
            for j in range(4):
                ff = fg * 4 + j
                for tt in range(4):
                    B.mm(accs[tt][:], act[:, ff, tt * 128:(tt + 1) * 128], wds[n][:, j, :], ff == 0,
                         ff == NFF - 1, [kwd, "act%d" % ff], ["bk%d" % (4 + tt)])
        for tt in range(4):
            n = B.alt("xs", [0, 1, 2, 3])
            kxs, kos = "xs%d" % n, "os%d" % n
            B.dma("sp", xss[n], src[tt * 128:(tt + 1) * 128, s * 512:(s + 1) * 512], [ksrc], [kxs])
            B.v("dve", "scalar_tensor_tensor", ["bk%d" % (4 + tt), kxs], [kos], out=oss[n],
                in0=accs[tt][:], scalar=0.5, in1=xss[n], op0=ALU.mult, op1=ALU.add)
            B.dma("sp", dst[tt * 128:(tt + 1) * 128, s * 512:(s + 1) * 512], oss[n], [kos], [kdst])


def attn_run(B, C, blocks, depth=2):
    groups = [blocks[i:i + 4] for i in range(0, len(blocks), 4)]
    pTs = C["pT"]
    nper, seen = {}, {}
    for b_ in blocks:
        nper[b_["ob"]] = nper.get(b_["ob"], 0) + 1

    def scores(gi):
        sbk = B.alt("scbank", SC_BANKS)
        bank = C["banks"][sbk]
        kb = "bk%d" % sbk
        grp = groups[gi]
        mms = []
        for bi, b_ in enumerate(grp):
            cs = slice(bi * 128, (bi + 1) * 128)
            mms.append((bank[:, cs], b_["kT"], b_["qT"], b_["rk"]))
            for (l, r, rk) in b_["extras"]:
                mms.append((bank[:, cs], l, r, rk))
        for i, (o, l, r, rk) in enumerate(mms):
            B.mm(o, l, r, i == 0, i == len(mms) - 1, rk, [kb])
        pi = B.alt("pT", [0, 1, 2])
        kp = "pT%d" % pi
        n = len(grp) * 128
        B.act(pTs[pi][:, 0:n], bank[:, 0:n], AF.Exp, [kb], [kp], scale=0.125)
        return pTs[pi], kp

    def pv(gi, pT, kp):
        for bi, b_ in enumerate(groups[gi]):
            c0, ncol = b_["oc"]
            ob = b_["ob"]
            k = seen.get(ob, 0)
            seen[ob] = k + 1
            B.mm(C["banks"][ob][:, c0:c0 + ncol], pT[:, bi * 128:(bi + 1) * 128], b_["v"], k == 0,
                 k == nper[ob] - 1, [kp] + b_["rk"], ["bk%d" % ob])

    pend = {}
    nxt = 0
    for gi in range(len(groups)):
        while nxt < len(groups) and nxt <= gi + depth:
            pend[nxt] = scores(nxt)
            nxt += 1
        pT, kp = pend.pop(gi)
        pv(gi, pT, kp)


def gelu_tanh(B, eng_keys, x, kx, tmp, kt, out, kout):
    B.v("dve", "tensor_tensor", [kx], [kt], out=tmp, in0=x, in1=x, op=ALU.mult)
    B.v("dve", "tensor_scalar", [kt], [kt], out=tmp, in0=tmp, scalar1=0.044715, scalar2=1.0,
        op0=ALU.mult, op1=ALU.add)
    B.v("dve", "tensor_tensor", [kt, kx], [kt], out=tmp, in0=tmp, in1=x, op=ALU.mult)
    B.act(tmp, tmp, AF.Sigmoid, [kt], [kt], scale=1.5957691216057308)
    B.v("dve", "tensor_tensor", [kt, kx], [kout], out=out, in0=tmp, in1=x, op=ALU.mult)


def mixer(B, C, L, src, dst, ksrc, kdst, W, K, SCR, stop=99, dbg=None):
    A = C["arena"]
    banks = C["banks"]
    uid = "m%d" % L
    vE = A.view(0, [16, 6, 66], BF16)
    gl = A.view(12672, [16, 24], F32)
    hT = A.view(14208, [16, 2048], BF16)
    yT = hT
    WS = 79744
    qkT, rgT = SCR["qkT"], SCR["rgT"]

    if stop <= -1:
        return
    pv_ = C["pvec"]
    B.dma("sp", pv_[:], W["pvec"][L], [], ["pvec"])
    B.dma("sp", C["esink"][:], W["sinks"][L], [], ["esink"])
    B.act(C["esink"][:], C["esink"][:], AF.Exp, ["esink"], ["esink"])
    gmix, kgm = load_gain(B, C, W["mix_norm"][L:L + 1, :], 0)
    ggn, kgg = load_gain(B, C, W["group_norm"][L:L + 1, 0:1024], 1)
    B.v("dve", "memset", [], ["vE"], vE[:, :, :, 64:65], 1.0)

    for t in range(16):
        xt = C["xt"][t % 2]
        kx = "xt%d" % (t % 2)
        B.dma("sp", xt[:], src[t * 128:(t + 1) * 128, :], [ksrc[t // 4]], [kx])
        norm_T(B, C, xt[:], kx, 2048, gmix[:, 0:2048], kgm, hT[:, :, t * 128:(t + 1) * 128], "hT", t % 2)

    if stop <= 0:
        return
    slabs = [A.view(WS + i * 16384, [16, 512], BF16) for i in range(2)]
    stg = [A.view(WS + 32768 + i * 2048, [512], F32) for i in range(4)]
    stgb = [A.view(WS + 32768 + i * 2048, [512], BF16) for i in range(4)]
    wqk = W["w_in_qk"][L].rearrange("(kc p) f -> p kc f", p=128)
    wrg = W["w_in"][L].rearrange("(kc p) f -> p kc f", p=128)
    wtm = W["w_in_tm"][L].rearrange("(kc p) f -> p kc f", p=128)
    import os as _os
    for si in range(int(_os.environ.get('M2_SLABS', '10'))):
        n = B.alt("wsl", [0, 1])
        ksl = "wsl%d" % n
        if si < 6:
            B.dma("pool", slabs[n], wqk[:, :, si * 512:(si + 1) * 512], [], [ksl])
        else:
            c0 = 2072 + (si - 6) * 512
            B.dma("pool", slabs[n], wrg[:, :, c0:c0 + 512], [], [ksl])
        for j in range(4):
            ch = si * 4 + j if si < 6 else (si - 6) * 4 + j
            if si == 5 and j >= 2:
                continue
            for tc in range(4):
                pb = B.alt("pjbank", [0, 1, 2, 3, 4, 5])
                kb = "bk%d" % pb
                for kc in range(16):
                    B.mm(banks[pb][:], slabs[n][:, kc, j * 128:(j + 1) * 128], hT[:, kc, tc * 512:(tc + 1) * 512],
                         kc == 0, kc == 15, [ksl, "hT"], [kb])
                sn = B.alt("stg", [0, 1, 2, 3])
                kst = "stg%d" % sn
                dst_sb = stgb[sn] if si < 6 else stg[sn]
                if B.alt("pjcopy", ["act", "dve"]) == "act":
                    B.act(dst_sb, banks[pb][:], AF.Copy, [kb], [kst])
                else:
                    B.v("dve", "tensor_copy", [kb], [kst], out=dst_sb, in_=banks[pb][:])
                if si < 6:
                    B.dma("sp", qkT[ch * 128:(ch + 1) * 128, tc * 512:(tc + 1) * 512], dst_sb, [kst], ["qk%d" % ch])
                else:
                    B.dma("sp", rgT[ch * 128:(ch + 1) * 128, tc * 512:(tc + 1) * 512], dst_sb, [kst], ["rg%d" % ch])
    n = B.alt("wsl", [0, 1])
    ksl = "wsl%d" % n
    if not _os.environ.get("TM_SKIP_DMA"):
        B.dma("pool", slabs[n][:, :, 0:408], wtm, [], [ksl])
    for t in range(int(_os.environ.get('M2_TM', '16'))):
        pb = B.alt("pjbank", [0, 1, 2, 3, 4, 5])
        kb = "bk%d" % pb
        for kc in range(16):
            B.mm(banks[pb][:, 0:408], hT[:, kc, t * 128:(t + 1) * 128], slabs[n][:, kc, 0:408], kc == 0, kc == 15,
                 [ksl, "hT"], [kb])
        if not _os.environ.get("TM_SKIP_ACT"):
            B.act(vE[:, t, :, 0:64], banks[pb][:, 0:384].rearrange("p (s d) -> p s d", s=6), AF.Copy, [kb], ["vE"])
        if not _os.environ.get("TM_SKIP_GL"):
            B.v("dve", "tensor_copy", [kb], ["gl"], out=gl[:, t, :], in_=banks[pb][:, 384:408])

    if stop <= 1:
        return
    B.barrier()
    f32v = [A.view(WS + i * 8192, [2048], F32) for i in range(6)]
    xcb = A.view(WS + 49152, [2048], BF16)
    wab = A.view(WS + 53248, [2, 128], BF16)
    sqlo = A.view(WS + 53760, [2048], BF16)
    oneb = C["oneb"]
    ssqC = banks[7]
    rgv = rgT.rearrange("(c p) t -> p c t", p=128)
    PVOFF = {"cw0": 0, "cw1": 8, "cw2": 16, "cw3": 24, "cb": 32, "ba": 40, "bx": 48, "lam": 56, "gc": 64}
    PVC = lambda name, cc: pv_[:, PVOFF[name] + cc:PVOFF[name] + cc + 1]
    nsp = C["nsp8"]
    B.act(nsp[:], pv_[:, 56:64], AF.Exp, ["pvec"], ["nsp8"], scale=-1.0)
    B.act(nsp[:], nsp[:], AF.Ln, ["nsp8", "onec"], ["nsp8"], bias=C["one"][:], scale=1.0)
    B.v("dve", "tensor_scalar", ["nsp8"], ["nsp8"], out=nsp[:], in0=nsp[:], scalar1=-8.0, scalar2=None, op0=ALU.mult)
    for cc in range(8):
        xr, xg, xc, r_, i_, t_ = f32v
        kxr, kxg, kxc, kr_, ki_, kt_ = ["f32v%d" % i for i in range(6)]
        B.dma("sp", xr, rgv[:, cc, :], ["rg%d" % cc], [kxr])
        B.dma("sp", xg, rgv[:, 8 + cc, :], ["rg%d" % (8 + cc)], [kxg])
        B.dma("pool", wab[:, 0, :], W["wabd"][L, cc], [], ["wab"])
        B.dma("pool", wab[:, 1, :], W["wxbd"][L, cc], [], ["wab"])
        B.v("dve", "tensor_scalar", [kxr, "pvec"], [kxc], out=xc, in0=xr, scalar1=PVC("cw3", cc),
            scalar2=PVC("cb", cc), op0=ALU.mult, op1=ALU.add)
        for j in range(3):
            sh = 3 - j
            B.v("dve", "scalar_tensor_tensor", [kxr, kxc, "pvec"], [kxc], out=xc[:, sh:], in0=xr[:, 0:S - sh],
                scalar=PVC("cw%d" % j, cc), in1=xc[:, sh:], op0=ALU.mult, op1=ALU.add)
        B.act(xcb, xc, AF.Copy, [kxc], ["xcb"])
        for gi, (dstg, kdg, bname) in enumerate([(r_, kr_, "ba"), (i_, ki_, "bx")]):
            for tc in range(4):
                pb = B.alt("cgbank", [0, 1, 2, 3])
                kb = "bk%d" % pb
                B.mm(banks[pb][:], wab[:, gi, :], xcb[:, tc * 512:(tc + 1) * 512], True, True, ["wab", "xcb"], [kb])
                B.act(dstg[:, tc * 512:(tc + 1) * 512], banks[pb][:], AF.Sigmoid, [kb, "pvec"], [kdg],
                      bias=PVC(bname, cc), scale=1.0)
        B.act(r_, r_, AF.Exp, [kr_, "nsp8"], [kr_], scale=nsp[:, cc:cc + 1])
        B.v("dve", "tensor_tensor", [kr_], [kt_], out=t_, in0=r_, in1=r_, op=ALU.mult)
        B.v("dve", "tensor_scalar", [kt_], [kt_], out=t_, in0=t_, scalar1=-1.0, scalar2=1.0, op0=ALU.mult, op1=ALU.add)
        B.act(t_, t_, AF.Sqrt, [kt_], [kt_])
        B.v("dve", "tensor_tensor", [kt_, ki_], [kt_], out=t_, in0=t_, in1=i_, op=ALU.mult)
        B.v("dve", "tensor_tensor", [kt_, kxc], [kt_], out=t_, in0=t_, in1=xc, op=ALU.mult)
        B.v("dve", "tensor_tensor_scan", [kr_, kt_], [ki_], out=i_, data0=r_, data1=t_, initial=0.0,
            op0=ALU.mult, op1=ALU.add)
        gelu_tanh(B, None, xg, kxg, t_, kt_, xr, kxr)
        B.v("dve", "tensor_tensor", [kxr, ki_], [kxc], out=xc, in0=xr, in1=i_, op=ALU.mult)
        B.act(t_, xc, AF.Square, [kxc], [kt_])
        B.v("dve", "tensor_copy", [kt_], ["xcb"], out=xcb, in_=t_)
        B.v("dve", "tensor_tensor", [kt_, "xcb"], [kt_], out=t_, in0=t_, in1=xcb, op=ALU.subtract)
        B.v("dve", "tensor_copy", [kt_], ["sqlo"], out=sqlo, in_=t_)
        for t in range(16):
            col = cc * 16 + t
            B.mm(ssqC[:, col:col + 1], xcb[:, t * 128:(t + 1) * 128], oneb[:, 0:1], True, False,
                 ["xcb", "onec"], ["bk7"])
            B.mm(ssqC[:, col:col + 1], sqlo[:, t * 128:(t + 1) * 128], oneb[:, 0:1], False, True,
                 ["sqlo", "onec"], ["bk7"])
        B.v("dve", "tensor_scalar", [kxc, "pvec"], ["yT"], out=yT[:, 8 + cc, :], in0=xc, scalar1=PVC("gc", cc),
            scalar2=None, op0=ALU.mult)
    rsC = C["rstdC"]
    B.v("dve", "tensor_reduce", ["bk7"], ["rstdC"], out=rsC[:], in_=ssqC[:, 0:128].rearrange("p (c t) -> p t c", c=8),
        axis=AX.X, op=ALU.add)
    B.act(rsC[:], rsC[:], AF.Sqrt, ["rstdC", "epsc"], ["rstdC"], bias=C["eps"][:], scale=1.0 / 1024)
    B.v("dve", "reciprocal", ["rstdC"], ["rstdC"], out=rsC[:], in_=rsC[:])

    if stop <= 2:
        return
    B.barrier()
    qkv = qkT.rearrange("(c p) t -> p c t", p=128)
    qaT = A.view(WS, [4, 2048], BF16)
    kaT = A.view(WS + 16384, [4, 2048], BF16)
    C["pT"] = [A.view(WS + 49152 + i * 1024, [512], BF16) for i in range(3)]
    ytile = [A.view(WS + 53248 + i * 2048, [8, 64], F32) for i in range(2)]
    B.dma("sp", qaT, qkv[:, 0:4, :], ["qk0", "qk1", "qk2", "qk3"], ["qaT"])
    B.dma("sp", kaT, qkv[:, 4:8, :], ["qk4", "qk5", "qk6", "qk7"], ["kaT"])
    identb, diagb, edgeb = K["identb"], K["diagb"], K["edgeb"]
    for qt in range(16):
        yt_ = ytile[qt % 2]
        kyt = "ytile%d" % (qt % 2)
        for kv in range(2):
            blocks = []
            kts = [qt - 1, qt] if qt > 0 else [qt]
            for g in range(4):
                h = kv * 4 + g
                base = 64 * (h % 2)
                for kt in kts:
                    mb = diagb if kt == qt else edgeb
                    blocks.append(dict(
                        kT=kaT[:, kv * 2 + h % 2, kt * 128:(kt + 1) * 128],
                        qT=qaT[:, h // 2, qt * 128:(qt + 1) * 128],
                        extras=[(identb[:], mb[:], ["kconst"])],
                        v=vE[:, kt, kv, 0:65], oc=(g * 65, 65), rk=["qaT", "kaT", "vE"], ob=0))
            ob = B.alt("obank", [3, 4, 5])
            kob = "bk%d" % ob
            for b_ in blocks:
                b_["ob"] = ob
            attn_run(B, C, blocks)
            ov = banks[ob][:, 0:260].rearrange("p (g c) -> p g c", c=65)
            den = C["den"][0]
            B.v("dve", "tensor_tensor", [kob, "esink"], ["den0"], out=den[:], in0=ov[:, :, 64],
                in1=C["esink"][:, kv * 4:(kv + 1) * 4], op=ALU.add)
            B.v("dve", "reciprocal", ["den0"], ["den0"], out=den[:], in_=den[:])
            B.v("dve", "tensor_tensor", [kob, "den0"], [kyt], out=yt_[:, kv * 4:(kv + 1) * 4, :], in0=ov[:, :, 0:64],
                in1=den[:].unsqueeze(2).broadcast_to([128, 4, 64]), op=ALU.mult)
        norm_T(B, C, yt_.rearrange("p h d -> p (h d)"), kyt, 512, ggn[:, 0:512], kgg,
               yT[:, 0:4, qt * 128:(qt + 1) * 128], "yT", qt % 2)

    if stop <= 3:
        return
    B.barrier()
    qbT = A.view(WS, [4, 2048], BF16)
    ksT = A.view(WS + 16384, [4, 2048], BF16)
    kwT = A.view(WS + 32768, [4, 2048], BF16)
    w1sb = A.view(WS + 16384, [32, 256], BF16)
    cxT = A.view(WS + 32768, [2048], BF16)
    cxz = [A.view(WS + 36864 + i * 4096, [2048], BF16) for i in range(2)]
    ctmp = [A.view(WS + 45056 + i * 512, [128], F32) for i in range(3)]
    B.dma("sp", qbT, qkv[:, 8:12, :], ["qk8", "qk9", "qk10", "qk11"], ["qbT"])
    KcT, VcE, hTb, w2sb, posT = C["KcT"], C["VcE"], C["hTb"], C["w2sb"], C["posT"]
    B.v("dve", "memset", [], ["KcT"], KcT[:], 0.0)
    B.v("dve", "memset", [], ["VcE"], VcE[:], 0.0)
    B.v("dve", "memset", ["VcE"], ["VcE"], VcE[:, :, 64:65], 1.0)
    B.dma("pool", C["ovl"][:], K["overlap_d"], [], ["ovl"])
    for kv in range(2):
        B.v("dve", "tensor_copy", ["ovl", "VcE"], ["VcE"], out=VcE[:, kv, 65:97], in_=C["ovl"][:])
    for i in range(2):
        B.dma("sp", cxT, qkv[:, 20 + i, :], ["qk%d" % (20 + i)], ["cxT"])
        for kv in range(2):
            B.v("dve", "memset", [], ["cxz%d" % kv], cxz[kv], 0.0)
            B.v("dve", "tensor_copy", ["cxT"], ["cxz%d" % kv], out=cxz[kv][64 * kv:64 * kv + 64, :],
                in_=cxT[64 * kv:64 * kv + 64, :])
        w1v = W["w1r"][L, i].rearrange("p (a b) -> p a b", a=32)
        for q4 in range(4):
            B.dma("pool", w1sb[:, q4 * 8:(q4 + 1) * 8, :], w1v[:, q4 * 8:(q4 + 1) * 8, :], [], ["w1sb"])
        B.dma("pool", w2sb[:].rearrange("p a b -> p (a b)"), W["w2r"][L, i], [], ["w2sb"])
        B.dma("pool", posT[:], W["posT"][L, i], [], ["posT"])
        if _os.environ.get("NSA_STOP") == "dma":
            continue
        hb = banks[0]
        first = True
        for kv in range(2):
            base = 64 * kv
            for jc in range(2):
                blk = kv * 2 + jc
                for l in range(32):
                    if _os.environ.get("NSA_MM") != "nomain":
                        B.mm(hb[:, blk * 128:blk * 128 + 127], w1sb[:, l, jc * 128:(jc + 1) * 128],
                             cxz[kv][:, l:l + 16 * 126 + 1:16], first, False, ["w1sb", "cxz%d" % kv], ["bk0"])
                        first = False
                    if _os.environ.get("NSA_MM") == "nopos":
                        continue
                    B.mm(hb[:, blk * 128 + 127:blk * 128 + 128], w1sb[:, l, jc * 128:(jc + 1) * 128],
                         posT[:, l:l + 1], False, (blk == 3 and l == 31), ["w1sb", "posT"], ["bk0"])
        if _os.environ.get("NSA_STOP") == "h":
            continue
        for blk in range(4):
            jc = blk % 2
            cb = C["cb"]
            B.v("dve", "tensor_tensor", ["bk0", "pvec"], ["cb"], out=cb[:], in0=hb[:, blk * 128 + 127:blk * 128 + 128],
                in1=pv_[:, 72 + 2 * i + jc:72 + 2 * i + jc + 1], op=ALU.add)
            B.v("dve", "tensor_scalar", ["bk0", "cb"], ["ctmp0"], out=ctmp[0][:, 0:127], in0=hb[:, blk * 128:blk * 128 + 127],
                scalar1=cb[:], scalar2=None, op0=ALU.add)
            gelu_tanh(B, None, ctmp[0][:, 0:127], "ctmp0", ctmp[1][:, 0:127], "ctmp1", hTb[:, blk, 0:127], "hTb")
        if _os.environ.get("NSA_STOP") == "g":
            continue
        ob = banks[1]
        if i == 0:
            for kv in range(2):
                for jc in range(2):
                    B.mm(ob[:, kv * 128:kv * 128 + 127], w2sb[:, jc, :], hTb[:, kv * 2 + jc, 0:127],
                         kv == 0 and jc == 0, kv == 1 and jc == 1, ["w2sb", "hTb"], ["bk1"])
            for kv in range(2):
                for par in range(2):
                    B.v("dve", "tensor_copy", ["bk1"], ["KcT"], out=KcT[64 * par:64 * par + 64, kv * 2 + par, 0:127],
                        in_=ob[64 * par:64 * par + 64, kv * 128:kv * 128 + 127])
        else:
            for kv in range(2):
                for jc in range(2):
                    B.mm(ob[0:127, kv * 64:(kv + 1) * 64], hTb[:, kv * 2 + jc, 0:127], w2sb[:, jc, 0:64],
                         kv == 0 and jc == 0, kv == 1 and jc == 1, ["w2sb", "hTb"], ["bk1"])
            B.v("dve", "tensor_copy", ["bk1"], ["VcE"], out=VcE[0:127, :, 0:64],
                in_=ob[0:127, 0:128].rearrange("p (k d) -> p k d", k=2))
    if _os.environ.get("NSA_STOP") in ("cmp", "dma", "h", "g"):
        return
    B.barrier()
    B.dma("sp", ksT, qkv[:, 12:16, :], ["qk12", "qk13", "qk14", "qk15"], ["ksT"])
    B.dma("sp", kwT, qkv[:, 16:20, :], ["qk16", "qk17", "qk18", "qk19"], ["kwT"])
    cmaskb, Eexp = K["cmaskb"], K["Eexp"]
    selA, selB = K["selA"], K["selB"]
    sgt, imp, imp2, imp3, m8, m8b, negb, negT = (C[k] for k in
                                                  ["sgt", "imp", "imp2", "imp3", "m8", "m8b", "negb", "negT"])
    tmpM = C["tmpM"]
    for qt in range(16):
        yt_ = ytile[qt % 2]
        kyt = "ytile%d" % (qt % 2)
        B.act(sgt[:], gl[:, qt, :], AF.Sigmoid, ["gl"], ["sgt"])
        for kv in range(2):
            blocks = []
            for g in range(4):
                h = kv * 4 + g
                blocks.append(dict(
                    kT=KcT[:, kv * 2 + h % 2, :],
                    qT=qbT[:, h // 2, qt * 128:(qt + 1) * 128],
                    extras=[(identb[:], cmaskb[:, qt * 128:(qt + 1) * 128], ["kconst"])],
                    v=VcE[:, kv, 0:97], oc=(g * 97, 97), rk=["qbT", "KcT", "VcE"], ob=3))
            for g in range(4):
                h = kv * 4 + g
                for kt in range(max(0, qt - 4), qt + 1):
                    ex = []
                    if kt == qt:
                        ex.append((identb[:], diagb[:], ["kconst"]))
                    if kt == qt - 4:
                        ex.append((identb[:], edgeb[:], ["kconst"]))
                    blocks.append(dict(
                        kT=kwT[:, kv * 2 + h % 2, kt * 128:(kt + 1) * 128],
                        qT=qbT[:, h // 2, qt * 128:(qt + 1) * 128],
                        extras=ex, v=vE[:, kt, 4 + kv, 0:65], oc=(g * 65, 65), rk=["qbT", "kwT", "vE"], ob=5))
            attn_run(B, C, blocks)
            ovc = banks[3][:, 0:388].rearrange("p (g c) -> p g c", c=97)
            dc, ds, dw = C["den"][0], C["den"][1], C["den"][2]
            B.v("dve", "tensor_scalar", ["bk3"], ["den0"], out=dc[:], in0=ovc[:, :, 64], scalar1=1e-30, scalar2=None,
                op0=ALU.add)
            B.v("dve", "reciprocal", ["den0"], ["den0"], out=dc[:], in_=dc[:])
            B.v("dve", "tensor_tensor", ["bk3", "den0"], ["tmpM"], out=tmpM[:], in0=ovc[:, :, 65:97],
                in1=dc[:].unsqueeze(2).broadcast_to([128, 4, 32]), op=ALU.mult)
            B.v("dve", "tensor_reduce", ["tmpM"], ["imp"], out=imp[:], in_=tmpM[:].rearrange("p g j -> p j g"),
                axis=AX.X, op=ALU.add)
            B.v("dve", "tensor_tensor", ["imp", "kconst"], ["imp2"], out=imp2[:], in0=imp[:], in1=selA[:, qt, :], op=ALU.mult)
            B.v("dve", "tensor_tensor", ["imp2", "kconst"], ["imp2"], out=imp2[:], in0=imp2[:], in1=selB[:, qt, :], op=ALU.add)
            B.v("dve", "max", ["imp2"], ["m8"], out=m8[:], in_=imp2[:])
            B.v("dve", "match_replace", ["imp2", "m8"], ["imp3"], out=imp3[:], in_to_replace=m8[:], in_values=imp2[:],
                imm_value=-3.0e38)
            B.v("dve", "max", ["imp3"], ["m8b"], out=m8b[:], in_=imp3[:])
            B.v("dve", "tensor_scalar", ["imp2", "m8b"], ["negb"], out=negb[:], in0=imp2[:], scalar1=m8b[:, 7:8],
                scalar2=None, op0=ALU.is_ge)
            B.v("dve", "tensor_scalar", ["negb"], ["negb"], out=negb[:], in0=negb[:], scalar1=-NEG, scalar2=NEG,
                op0=ALU.mult, op1=ALU.add)
            B.tr(banks[6][0:32, 0:128], negb[:], C["identf"][:], ["negb", "ident"], ["bk6"])
            B.v("dve", "tensor_copy", ["bk6"], ["negT"], out=negT[0:32, :], in_=banks[6][0:32, 0:128])
            blocks = []
            for g in range(4):
                h = kv * 4 + g
                for kt in range(qt + 1):
                    ex = [(Eexp[:, kt, :], negT[:], ["kconst", "negT"])]
                    if kt == qt:
                        ex.append((identb[:], diagb[:], ["kconst"]))
                    blocks.append(dict(
                        kT=ksT[:, kv * 2 + h % 2, kt * 128:(kt + 1) * 128],
                        qT=qbT[:, h // 2, qt * 128:(qt + 1) * 128],
                        extras=ex, v=vE[:, kt, 2 + kv, 0:65], oc=(g * 65, 65), rk=["qbT", "ksT", "vE"], ob=4))
            attn_run(B, C, blocks)
            ovs = banks[4][:, 0:260].rearrange("p (g c) -> p g c", c=65)
            ovw = banks[5][:, 0:260].rearrange("p (g c) -> p g c", c=65)
            B.v("dve", "reciprocal", ["bk4"], ["den1"], out=ds[:], in_=ovs[:, :, 64])
            B.v("dve", "reciprocal", ["bk5"], ["den2"], out=dw[:], in_=ovw[:, :, 64])
            B.v("dve", "tensor_tensor", ["den0", "sgt"], ["den0"], out=dc[:], in0=dc[:], in1=sgt[:, kv * 4:kv * 4 + 4], op=ALU.mult)
            B.v("dve", "tensor_tensor", ["den1", "sgt"], ["den1"], out=ds[:], in0=ds[:], in1=sgt[:, 8 + kv * 4:8 + kv * 4 + 4], op=ALU.mult)
            B.v("dve", "tensor_tensor", ["den2", "sgt"], ["den2"], out=dw[:], in0=dw[:], in1=sgt[:, 16 + kv * 4:16 + kv * 4 + 4], op=ALU.mult)
            ysl = yt_[:, kv * 4:(kv + 1) * 4, :]
            ot = C["otmp"]
            B.v("dve", "tensor_tensor", ["bk3", "den0"], [kyt], out=ysl, in0=ovc[:, :, 0:64],
                in1=dc[:].unsqueeze(2).broadcast_to([128, 4, 64]), op=ALU.mult)
            B.v("dve", "tensor_tensor", ["bk4", "den1"], ["otmp"], out=ot[:], in0=ovs[:, :, 0:64],
                in1=ds[:].unsqueeze(2).broadcast_to([128, 4, 64]), op=ALU.mult)
            B.v("dve", "tensor_tensor", ["otmp", kyt], [kyt], out=ysl, in0=ysl, in1=ot[:], op=ALU.add)
            B.v("dve", "tensor_tensor", ["bk5", "den2"], ["otmp"], out=ot[:], in0=ovw[:, :, 0:64],
                in1=dw[:].unsqueeze(2).broadcast_to([128, 4, 64]), op=ALU.mult)
            B.v("dve", "tensor_tensor", ["otmp", kyt], [kyt], out=ysl, in0=ysl, in1=ot[:], op=ALU.add)
        norm_T(B, C, yt_.rearrange("p h d -> p (h d)"), kyt, 512, ggn[:, 512:1024], kgg,
               yT[:, 4:8, qt * 128:(qt + 1) * 128], "yT", qt % 2)

    if dbg is not None:
        B.dma("sp", dbg.rearrange("(c p) t -> p c t", p=128), yT, ["yT"], ["dbg"])
    if stop <= 4:
        return
    B.barrier()
    wos = [A.view(WS + i * 16384, [16, 512], BF16) for i in range(2)]
    xss = [A.view(WS + 32768 + i * 2048, [512], F32) for i in range(4)]
    oss = [A.view(WS + 40960 + i * 2048, [512], F32) for i in range(4)]
    wov = W["w_out"][L].rearrange("(kc p) d -> p kc d", p=128)
    for s in range(4):
        n = B.alt("wos", [0, 1])
        kwo = "wos%d" % n
        B.dma("pool", wos[n], wov[:, :, s * 512:(s + 1) * 512], [], [kwo])
        for tt in range(16):
            pa = B.alt("wobank", [0, 2, 4])
            pc = pa + 1
            for kc in range(8):
                B.mm(banks[pa][:], yT[:, kc, tt * 128:(tt + 1) * 128], wos[n][:, kc, :], kc == 0, kc == 7,
                     ["yT", kwo], ["bk%d" % pa])
            for kc in range(8, 16):
                B.mm(banks[pc][:], yT[:, kc, tt * 128:(tt + 1) * 128], wos[n][:, kc, :], kc == 8, kc == 15,
                     ["yT", kwo], ["bk%d" % pc])
            m = B.alt("xso", [0, 1, 2, 3])
            kxs, kos = "mxs%d" % m, "mos%d" % m
            B.dma("sp", xss[m], src[tt * 128:(tt + 1) * 128, s * 512:(s + 1) * 512], [ksrc[tt // 4]], [kxs])
            B.v("dve", "tensor_tensor", ["bk%d" % pa, kxs], [kxs], out=xss[m], in0=banks[pa][:], in1=xss[m], op=ALU.add)
            B.v("dve", "scalar_tensor_tensor", ["bk%d" % pc, kxs, "rstdC"], [kos], out=oss[m], in0=banks[pc][:],
                scalar=rsC[:, tt:tt + 1], in1=xss[m], op0=ALU.mult, op1=ALU.add)
            B.dma("sp", dst[tt * 128:(tt + 1) * 128, s * 512:(s + 1) * 512], oss[m], [kos], [kdst[tt // 4]])


def _dup(a, b):
    return list(range(a, b)) * 2


def _top(a, b):
    return list(range(a, b)) + [-1] * 64


def _bot(a, b):
    return [-1] * 64 + list(range(a, b))


QK_COLS = (list(range(0, 512)) + _top(512, 576) + _bot(512, 576) + _top(576, 640) + _bot(576, 640) +
           list(range(768, 1280)) + _top(1536, 1600) + _bot(1536, 1600) + _top(1600, 1664) + _bot(1600, 1664) +
           _top(1792, 1856) + _bot(1792, 1856) + _top(1856, 1920) + _bot(1856, 1920) +
           list(range(1280, 1408)) + list(range(1408, 1536)) + [-1] * 256)
NQK = len(QK_COLS)
TM_COLS = list(range(640, 768)) + list(range(1664, 1792)) + list(range(1920, 2048)) + list(range(2048, 2072))


def host_consts():
    k = np.arange(128)[:, None]
    q = np.arange(128)[None, :]
    c = {}
    c["identf_d"] = np.eye(128, dtype=np.float32)
    c["diag_d"] = np.where(k <= q, 0.0, NEG).astype(np.float32)
    c["edge_d"] = np.where(k > q, 0.0, NEG).astype(np.float32)
    cc = np.arange(128)[:, None]
    qq = np.arange(S)[None, :]
    c["cmask_d"] = np.where((cc < 127) & (16 * cc + 31 <= qq), 0.0, NEG).astype(np.float32)
    E = np.zeros((128, 16, 128), np.float32)
    for kt in range(16):
        for kk in range(128):
            E[2 * kt + kk // 64, kt, kk] = 1.0
    c["eexp_d"] = E.reshape(128, 2048)
    qpos = np.arange(S)[:, None]
    j = np.arange(32)[None, :]
    cur = qpos // 64
    forced = (j == 0) | (j == cur) | (j == cur - 1)
    valid = j * 64 <= qpos
    selA = np.where(forced | ~valid, 0.0, 1.0).astype(np.float32)
    selB = np.where(~valid, -1e30, np.where(forced, 1e4, 0.0)).astype(np.float32)
    c["selA_d"] = np.ascontiguousarray(selA.reshape(16, 128, 32).transpose(1, 0, 2)).reshape(128, 512)
    c["selB_d"] = np.ascontiguousarray(selB.reshape(16, 128, 32).transpose(1, 0, 2)).reshape(128, 512)
    cs = np.arange(127)[:, None] * 16
    ss = np.arange(32)[None, :] * 64
    ov = np.clip(np.minimum(cs + 32, ss + 64) - np.maximum(cs, ss), 0, None) / 32.0
    o = np.zeros((128, 32), np.float32)
    o[:127] = ov
    c["overlap_d"] = o
    return c


def host_layout(inp):
    f = lambda a: np.ascontiguousarray(np.asarray(a, dtype=np.float32))
    d = {}
    for kk in ["ffn1_w_gate", "ffn1_w_up", "ffn1_w_down", "ffn2_w_gate", "ffn2_w_up", "ffn2_w_down", "w_in",
               "w_out", "ffn1_norm", "ffn2_norm", "mix_norm", "group_norm"]:
        d[kk] = f(inp[kk])
    d["final_norm"] = f(inp["final_norm"]).reshape(1, D)
    w_in = d["w_in"]
    cols = np.asarray(QK_COLS)
    wqk = np.zeros((DEPTH, D, NQK), np.float32)
    wqk[:, :, cols >= 0] = w_in[:, :, cols[cols >= 0]]
    d["w_in_qk"] = wqk
    d["w_in_tm"] = f(w_in[:, :, TM_COLS])
    pvec = np.zeros((DEPTH, 128, NPV), np.float32)
    cw, cb = f(inp["conv_w"]), f(inp["conv_b"])
    for l in range(DEPTH):
        for j in range(4):
            pvec[l, :, j * 8:(j + 1) * 8] = cw[l, j].reshape(8, 128).T
        pvec[l, :, 32:40] = cb[l].reshape(8, 128).T
        pvec[l, :, 40:48] = f(inp["lru_ba"])[l].reshape(8, 128).T
        pvec[l, :, 48:56] = f(inp["lru_bx"])[l].reshape(8, 128).T
        pvec[l, :, 56:64] = f(inp["lru_lambda"])[l].reshape(8, 128).T
        pvec[l, :, 64:72] = d["group_norm"][l, 1024:].reshape(8, 128).T
        for i in range(2):
            pvec[l, :, 72 + 2 * i:74 + 2 * i] = f(inp["cmp_b1"])[l, i].reshape(2, 128).T
    d["pvec"] = pvec
    d["sinks"] = f(np.broadcast_to(f(inp["swa_sinks"])[:, None, :], (DEPTH, 128, 8)))
    pos = f(inp["cmp_pos"])
    posT = np.zeros((DEPTH, 2, 128, 32), np.float32)
    posT[:, :, 0:64, :] = pos.transpose(0, 1, 3, 2)
    d["posT"] = posT
    w1 = f(inp["cmp_w1"]).reshape(DEPTH, 2, 32, 64, 256).transpose(0, 1, 3, 2, 4)
    d["w1r"] = f(np.tile(w1, (1, 1, 2, 1, 1))).reshape(DEPTH, 2, 128, 32 * 256)
    w2 = f(inp["cmp_w2"]).reshape(DEPTH, 2, 2, 128, 64).transpose(0, 1, 3, 2, 4)
    d["w2r"] = f(np.tile(w2, (1, 1, 1, 1, 2))).reshape(DEPTH, 2, 128, 256)
    for nm, src in [("wabd", "lru_wa"), ("wxbd", "lru_wx")]:
        w = f(inp[src])
        bd = np.zeros((DEPTH, 8, 128, 128), np.float32)
        for cc in range(8):
            bd[:, cc, 0:64, 0:64] = w[:, 2 * cc]
            bd[:, cc, 64:128, 64:128] = w[:, 2 * cc + 1]
        d[nm] = bd
    d.update(host_consts())
    return d


IN_SHAPES = {
    "ffn1_w_gate": [DEPTH, D, DFF], "ffn1_w_up": [DEPTH, D, DFF], "ffn1_w_down": [DEPTH, DFF, D],
    "ffn2_w_gate": [DEPTH, D, DFF], "ffn2_w_up": [DEPTH, D, DFF], "ffn2_w_down": [DEPTH, DFF, D],
    "w_in": [DEPTH, D, DIN], "w_out": [DEPTH, D, D], "ffn1_norm": [DEPTH, D], "ffn2_norm": [DEPTH, D],
    "mix_norm": [DEPTH, D], "group_norm": [DEPTH, D], "final_norm": [1, D],
    "w_in_qk": [DEPTH, D, NQK], "w_in_tm": [DEPTH, D, 408], "pvec": [DEPTH, 128, NPV], "sinks": [DEPTH, 128, 8],
    "posT": [DEPTH, 2, 128, 32], "w1r": [DEPTH, 2, 128, 8192], "w2r": [DEPTH, 2, 128, 256],
    "wabd": [DEPTH, 8, 128, 128], "wxbd": [DEPTH, 8, 128, 128],
    "identf_d": [128, 128], "diag_d": [128, 128], "edge_d": [128, 128], "cmask_d": [128, 2048],
    "eexp_d": [128, 2048], "selA_d": [128, 512], "selB_d": [128, 512], "overlap_d": [128, 32],
}


def setup_common(B, W):
    C = {}
    C["identf"] = B.sb("identf", [128, 128], F32)
    C["eps"] = B.sb("eps", [128, 1], F32)
    C["one"] = B.sb("one", [128, 1], F32)
    C["onef"] = B.sb("onef", [128, 2], F32)
    C["oneb"] = B.sb("oneb", [128, 2], BF16)
    C["junk"] = B.sb("junk", [128, 2048], BF16)
    C["ssq"] = [B.sb("ssq%d" % i, [128, 1], F32) for i in range(2)]
    C["rstd"] = [B.sb("rstd%d" % i, [128, 1], F32) for i in range(2)]
    C["xn"] = [B.sb("xn%d" % i, [128, 2048], F32) for i in range(2)]
    C["xt"] = [B.sb("xt%d" % i, [128, 2048], F32) for i in range(2)]
    C["gbc"] = [B.sb("gbc0", [128, 2048], F32), B.sb("gbc1", [128, 1024], F32)]
    C["pvec"] = B.sb("pvec", [128, NPV], F32)
    C["esink"] = B.sb("esink", [128, 8], F32)
    C["nsp8"] = B.sb("nsp8", [128, 8], F32)
    C["rstdC"] = B.sb("rstdC", [128, 16], F32)
    C["den"] = [B.sb("den%d" % i, [128, 4], F32) for i in range(3)]
    C["KcT"] = B.sb("KcT", [128, 4, 128], BF16)
    C["VcE"] = B.sb("VcE", [128, 2, 98], BF16)
    C["hTb"] = B.sb("hTb", [128, 4, 128], BF16)
    C["w2sb"] = B.sb("w2sb", [128, 2, 128], BF16)
    C["posT"] = B.sb("posT", [128, 32], BF16)
    C["ovl"] = B.sb("ovl", [128, 32], BF16)
    C["cb"] = B.sb("cbias", [128, 1], F32)
    for nm, shp in [("sgt", [128, 24]), ("imp", [128, 32]), ("imp2", [128, 32]), ("imp3", [128, 32]),
                    ("m8", [128, 8]), ("m8b", [128, 8]), ("negb", [128, 32]), ("tmpM", [128, 4, 32]),
                    ("otmp", [128, 4, 64])]:
        C[nm] = B.sb(nm, shp, F32)
    C["negT"] = B.sb("negT", [128, 128], BF16)
    K = {}
    K["identb"] = B.sb("identb", [128, 128], BF16)
    K["diagb"] = B.sb("diagb", [128, 128], BF16)
    K["edgeb"] = B.sb("edgeb", [128, 128], BF16)
    K["cmaskb"] = B.sb("cmaskb", [128, 2048], BF16)
    K["Eexp"] = B.sb("Eexp", [128, 16, 128], BF16)
    K["selA"] = B.sb("selA", [128, 16, 32], F32)
    K["selB"] = B.sb("selB", [128, 16, 32], F32)
    K["overlap_d"] = W["overlap_d"]
    C["arena"] = Arena(B)
    C["banks"] = [B.ps("bank%d" % i, [128, 512], F32) for i in range(8)]
    B.dma("sp", C["identf"][:], W["identf_d"], [], ["ident"])
    B.dma("pool", K["identb"][:], W["identf_d"], [], ["kconst"])
    B.dma("pool", K["diagb"][:], W["diag_d"], [], ["kconst"])
    B.dma("pool", K["edgeb"][:], W["edge_d"], [], ["kconst"])
    B.dma("pool", K["cmaskb"][:], W["cmask_d"], [], ["kconst"])
    B.dma("pool", K["Eexp"][:].rearrange("p a b -> p (a b)"), W["eexp_d"], [], ["kconst"])
    B.dma("sp", K["selA"][:].rearrange("p a b -> p (a b)"), W["selA_d"], [], ["kconst"])
    B.dma("sp", K["selB"][:].rearrange("p a b -> p (a b)"), W["selB_d"], [], ["kconst"])
    B.v("dve", "memset", [], ["epsc"], C["eps"][:], RMS_EPS)
    B.v("dve", "memset", [], ["negT"], C["negT"][:], 0.0)
    B.v("dve", "memset", [], ["onec"], C["one"][:], 1.0)
    B.v("dve", "memset", [], ["onec"], C["onef"][:], 1.0)
    B.v("dve", "memset", [], ["onec"], C["oneb"][:], 1.0)
    return C, K


def final_norm(B, C, src, ksrc, dst, kdst, g_bc, kg):
    for t in range(16):
        xt = C["xt"][t % 2]
        kx = "xt%d" % (t % 2)
        B.dma("sp", xt[:], src[t * 128:(t + 1) * 128, :], [ksrc[t // 4]], [kx])
        xn, kn = norm_T(B, C, xt[:], kx, 2048, g_bc[:, 0:2048], kg, None, None, t % 2, transpose=False)
        B.dma("sp", dst[t * 128:(t + 1) * 128, :], xn[:], [kn], [kdst])


def build_kernel():
    B = Builder()
    nc = B.nc
    W = {}
    for nm, shp in IN_SHAPES.items():
        W[nm] = nc.dram_tensor(nm, shp, F32, kind="ExternalInput").ap()
    x = nc.dram_tensor("x", [S, D], F32, kind="ExternalInput").ap()
    out = nc.dram_tensor("out", [S, D], F32, kind="ExternalOutput").ap()
    xa = B.dram("xa", [S, D], F32)
    xb = B.dram("xb", [S, D], F32)
    SCR = {"qkT": B.dram("qkT", [NQK, 2048], BF16), "rgT": B.dram("rgT", [2048, 2048], F32)}
    C, K = setup_common(B, W)
    cur, kcur = x, ["x%d" % c for c in range(4)]
    pp = [(xa, "xa"), (xb, "xb")]
    nxt = 0

    def ffn_block(pre, L, cur, kcur, dst, kd):
        g, kg = load_gain(B, C, W[pre + "_norm"][L:L + 1, :], 0)
        for c in range(4):
            ffn_chunk(B, C, cur[c * 512:(c + 1) * 512, :], dst[c * 512:(c + 1) * 512, :], kcur[c], kd[c], g, kg,
                      W[pre + "_w_gate"][L], W[pre + "_w_up"][L], W[pre + "_w_down"][L], "%s_%d" % (pre, L))
        B.barrier()

    for L in range(DEPTH):
        dst, kd = pp[nxt][0], ["%s_%d_%d_%d" % (pp[nxt][1], L, 0, c) for c in range(4)]
        ffn_block("ffn1", L, cur, kcur, dst, kd)
        cur, kcur, nxt = dst, kd, 1 - nxt
        dst, kd = pp[nxt][0], ["%s_%d_%d_%d" % (pp[nxt][1], L, 1, c) for c in range(4)]
        mixer(B, C, L, cur, dst, kcur, kd, W, K, SCR)
        B.barrier()
        cur, kcur, nxt = dst, kd, 1 - nxt
        dst, kd = pp[nxt][0], ["%s_%d_%d_%d" % (pp[nxt][1], L, 2, c) for c in range(4)]
        ffn_block("ffn2", L, cur, kcur, dst, kd)
        cur, kcur, nxt = dst, kd, 1 - nxt
    g, kg = load_gain(B, C, W["final_norm"], 0)
    final_norm(B, C, cur, kcur, out, "out", g, kg)
    B.S.add("sp", None, ["out"], [])
    B.S.emit()
    return B


_CACHE = {}


def kernel(**inputs):
    x = np.ascontiguousarray(np.asarray(inputs["x"], dtype=np.float32))
    shared = host_layout(inputs)
    if "B" not in _CACHE:
        _CACHE["B"] = build_kernel()
    B = _CACHE["B"]
    in_maps = []
    for c in range(8):
        m = {nm: shared[nm] for nm in IN_SHAPES}
        m["x"] = x[c]
        in_maps.append(m)
    res = run_bass_kernel_spmd(B.nc, in_maps, core_ids=list(range(8)))
    return np.stack([np.asarray(r["out"], dtype=np.float32).reshape(S, D) for r in res.results], axis=0)
```

```python
import numpy as np
from contextlib import ExitStack
import concourse.bass as bass
import concourse.mybir as mybir
from concourse.bass_utils import run_bass_kernel_spmd

F32 = mybir.dt.float32
BF16 = mybir.dt.bfloat16
I32 = mybir.dt.int32
AF = mybir.ActivationFunctionType
ALU = mybir.AluOpType
AX = mybir.AxisListType

D = 2048
S = 2048
DEPTH = 2
DFF = 5632
NFF = DFF // 128
DIN = 4120
RMS_EPS = 1e-6
SEM_CAP = 30000


class _Op:
    __slots__ = ("eng", "fn", "r", "w", "dma", "deps", "awaited", "sem", "semval")

    def __init__(self, eng, fn, r, w, dma):
        self.eng = eng
        self.fn = fn
        self.r = tuple(r)
        self.w = tuple(w)
        self.dma = dma
        self.deps = None
        self.awaited = False
        self.sem = None
        self.semval = 0


class Sched:
    def __init__(self, nc, es, n_dma_sems=20):
        self.nc = nc
        self.es = es
        self.ops = []
        self.ENG = {"pe": nc.tensor, "act": nc.scalar, "dve": nc.vector,
                    "pool": nc.gpsimd, "sp": nc.sync}
        self.n_dma_sems = n_dma_sems
        self._semh = {}
        self.cur_barrier = None

    def add(self, eng, fn, r=(), w=(), dma=False):
        w = list(w) + [k for k in r if k.startswith("bk")]
        r = [k for k in r if not k.startswith("bk")]
        op = _Op(eng, fn, r, w, dma)
        op.deps = set()
        if self.cur_barrier is not None:
            op.deps.add(self.cur_barrier)
        self.ops.append(op)

    def barrier(self, fn):
        op = _Op("dve", fn, (), (), False)
        op.deps = set()
        start = self.cur_barrier if self.cur_barrier is not None else 0
        last = {}
        for i in range(start, len(self.ops)):
            o = self.ops[i]
            if o.dma:
                op.deps.add(i)
            else:
                last[o.eng] = i
        op.deps.update(last.values())
        self.ops.append(op)
        self.cur_barrier = len(self.ops) - 1

    def _sem(self, name):
        if name not in self._semh:
            self._semh[name] = self.es.enter_context(self.nc.semaphore(name))
        return self._semh[name]

    def emit(self):
        ops = self.ops
        last_w, rd_eng, rd_dma = {}, {}, {}
        for i, op in enumerate(ops):
            deps = op.deps
            for k in op.r:
                if k in last_w:
                    deps.add(last_w[k])
            for k in op.w:
                if k in last_w:
                    deps.add(last_w[k])
                deps.update(rd_eng.get(k, {}).values())
                deps.update(rd_dma.get(k, ()))
            deps.discard(i)
            op.deps = deps
            for k in op.r:
                if op.dma:
                    rd_dma.setdefault(k, []).append(i)
                else:
                    rd_eng.setdefault(k, {})[op.eng] = i
            for k in op.w:
                last_w[k] = i
                rd_eng[k] = {}
                rd_dma[k] = []
        rr, last_user, dcount = {}, {}, {}
        for i, op in enumerate(ops):
            if op.dma:
                n = rr.get(op.eng, 0)
                rr[op.eng] = n + 1
                s = "d_%s_%d" % (op.eng, n % self.n_dma_sems)
                if s in last_user:
                    op.deps.add(last_user[s])
                last_user[s] = i
                dcount[s] = dcount.get(s, 0) + 1
                op.sem = s
                op.semval = 16 * dcount[s]
        for op in ops:
            for j in op.deps:
                d = ops[j]
                if d.dma:
                    continue
                if d.eng == op.eng and op.eng == "pe" and not op.dma:
                    continue
                d.awaited = True
        cnt = {}
        for op in ops:
            if not op.dma and op.awaited:
                c = cnt.get(op.eng, 0)
                cnt[op.eng] = c + 1
                op.sem = "e_%s_%d" % (op.eng, c // SEM_CAP)
                op.semval = c % SEM_CAP + 1
        waited = {e: {} for e in self.ENG}
        nwait = 0
        for op in ops:
            e = op.eng
            need = {}
            for j in op.deps:
                d = ops[j]
                if (not d.dma) and d.eng == e and e == "pe" and not op.dma:
                    continue
                if need.get(d.sem, 0) < d.semval:
                    need[d.sem] = d.semval
            for s, v in need.items():
                if waited[e].get(s, 0) < v:
                    self.ENG[e].wait_ge(self._sem(s), v)
                    waited[e][s] = v
                    nwait += 1
            if op.fn is None:
                continue
            ins = op.fn()
            if op.dma:
                ins.then_inc(self._sem(op.sem), 16)
            elif op.awaited:
                ins.then_inc(self._sem(op.sem), 1)
        self.stats = dict(n_ops=len(ops), n_wait=nwait, awaited=dict(cnt))


class Builder:
    def __init__(self):
        self.nc = bass.Bass("TRN2", target_bir_lowering=False)
        self.es = ExitStack()
        self.S = Sched(self.nc, self.es)
        self._n = 0
        self.rr = {}
        self.bar_tile = self.sb("bar_tile", [128, 1], F32)

    def sb(self, name, shape, dt):
        return self.es.enter_context(self.nc.sbuf_tensor("s_" + name, list(shape), dt))

    def ps(self, name, shape, dt):
        return self.es.enter_context(self.nc.psum_tensor("p_" + name, list(shape), dt))

    def dram(self, name, shape, dt, kind="Internal"):
        return self.nc.dram_tensor("d_" + name, list(shape), dt, kind=kind).ap()

    def dma(self, q, out, in_, r, w, **kw):
        eng = self.S.ENG[q]
        self.S.add(q, lambda: eng.dma_start(out=out, in_=in_, **kw), r, w, dma=True)

    def mm(self, out, lhsT, rhs, start, stop, r, w):
        nc = self.nc
        self.S.add("pe", lambda: nc.tensor.matmul(out, lhsT, rhs, start=start, stop=stop), r, w)

    def tr(self, out, in_, ident, r, w):
        nc = self.nc
        self.S.add("pe", lambda: nc.tensor.transpose(out, in_, ident), r, w)

    def act(self, out, in_, func, r, w, bias=None, scale=None, accum_out=None):
        nc = self.nc
        kw = {}
        if bias is not None:
            kw["bias"] = bias
        if scale is not None:
            kw["scale"] = scale
        if accum_out is not None:
            kw["accum_out"] = accum_out
        self.S.add("act", lambda: nc.scalar.activation(out=out, in_=in_, func=func, **kw), r, w)

    def v(self, eng, name, r, w, *a, **kw):
        e = self.S.ENG[eng]
        self.S.add(eng, lambda: getattr(e, name)(*a, **kw), r, w)

    def barrier(self):
        nc = self.nc
        bt = self.bar_tile
        self.S.barrier(lambda: nc.vector.memset(bt[:], 0.0))

    def alt(self, key, choices):
        n = self.rr.get(key, 0)
        self.rr[key] = n + 1
        return choices[n % len(choices)]


import os as _os0
_NODMA = bool(_os0.environ.get('FFN_NODMA'))
NEG = -30000.0
ARENA_BYTES = 143360
SC_BANKS = [0, 1, 2]
NPV = 76


class Arena:
    def __init__(self, B):
        self.t = B.sb("arena", [128, ARENA_BYTES // 4], F32)

    def view(self, off, shape, dt, nparts=128):
        esz = 4 if dt == F32 else 2
        n = 1
        for d in shape:
            n *= d
        assert off % 4 == 0 and (n * esz) % 4 == 0 and off + n * esz <= ARENA_BYTES, (off, shape)
        v = self.t[0:nparts, off // 4:(off + n * esz) // 4]
        if dt != F32:
            v = v.bitcast(dt)
        if len(shape) == 2:
            v = v.rearrange("p (a b) -> p a b", a=shape[0])
        elif len(shape) == 3:
            v = v.rearrange("p (a b c) -> p a b c", a=shape[0], b=shape[1])
        return v


def norm_part(B, C, xt, kx, width, g_bc, kg, slot):
    junk, ssq, rstd, xn = C["junk"], C["ssq"][slot], C["rstd"][slot], C["xn"][slot]
    kj, ks, kr, kn = "junk", "ssq%d" % slot, "rstd%d" % slot, "xn%d" % slot
    B.act(junk[:, 0:width], xt, AF.Square, [kx], [kj, ks], accum_out=ssq[:])
    B.act(rstd[:], ssq[:], AF.Sqrt, [ks, "epsc"], [kr], bias=C["eps"][:], scale=1.0 / width)
    B.v("dve", "reciprocal", [kr], [kr], out=rstd[:], in_=rstd[:])
    B.v("dve", "scalar_tensor_tensor", [kx, kr, kg], [kn], out=xn[:, 0:width], in0=xt, scalar=rstd[:],
        in1=g_bc, op0=ALU.mult, op1=ALU.mult)
    return xn, kn


def tr_part(B, C, xn, kn, width, dstT, kdst, trbanks=(6, 7)):
    for q4 in range(width // 512):
        pb = B.alt("trbank%s" % (trbanks,), list(trbanks))
        pt = C["banks"][pb]
        kp = "bk%d" % pb
        for j in range(4):
            kc = q4 * 4 + j
            B.tr(pt[:, j * 128:(j + 1) * 128], xn[:, kc * 128:(kc + 1) * 128], C["identf"][:],
                 [kn, "ident"], [kp])
        eng = B.alt("trcopy", ["act", "dve"])
        src3 = pt[:].rearrange("p (j c) -> p j c", j=4)
        if eng == "act":
            B.act(dstT[:, q4 * 4:(q4 + 1) * 4, :], src3, AF.Copy, [kp], [kdst])
        else:
            B.v("dve", "tensor_copy", [kp], [kdst], out=dstT[:, q4 * 4:(q4 + 1) * 4, :], in_=src3)


def norm_T(B, C, xt, kx, width, g_bc, kg, dstT, kdst, slot, transpose=True):
    xn, kn = norm_part(B, C, xt, kx, width, g_bc, kg, slot)
    if not transpose:
        return xn, kn
    tr_part(B, C, xn, kn, width, dstT, kdst)


def load_gain(B, C, gvec, slot):
    n = gvec.shape[1]
    B.dma("sp", C["gbc"][slot][:, 0:n], gvec.partition_broadcast(128), [], ["gbc%d" % slot])
    return C["gbc"][slot], "gbc%d" % slot


def ffn_views(C):
    A = C["arena"]
    V = {}
    V["xnT"] = [A.view(0, [16, 512], BF16)]
    V["act"] = A.view(16384, [NFF, 512], BF16)
    V["wgs"] = [A.view(61440 + i * 8192, [16, 256], BF16) for i in range(3)]
    V["wus"] = [A.view(86016 + i * 8192, [16, 256], BF16) for i in range(3)]
    V["wds"] = [A.view(110592 + i * 4096, [4, 512], BF16) for i in range(3)]
    V["sgs"] = [A.view(122880 + i * 2048, [512], F32) for i in range(2)]
    V["xss"] = [A.view(126976 + i * 2048, [512], F32) for i in range(4)]
    V["oss"] = [A.view(135168 + i * 2048, [512], F32) for i in range(4)]
    return V


def ffn_prep_norm(B, C, V, src, ksrc, tiles, g_bc, kg):
    for t in tiles:
        xt = C["xt"][t % 2]
        kx = "xt%d" % (t % 2)
        B.dma("sp", xt[:], src[t * 128:(t + 1) * 128, :], [ksrc], [kx])
        norm_part(B, C, xt[:], kx, 2048, g_bc[:, 0:2048], kg, t % 2)


def ffn_prep_tr(B, C, V, tiles):
    xnT = V["xnT"][0]
    for t in tiles:
        tr_part(B, C, C["xn"][t % 2], "xn%d" % (t % 2), 2048, xnT[:, :, t * 128:(t + 1) * 128], "xnT",
                trbanks=(0, 1, 2, 3))


def ffn_gate_up(B, C, V, wg, wu, uid):
    xnT, act = V["xnT"][0], V["act"]
    wgv = wg.rearrange("(kc p) f -> p kc f", p=128)
    wuv = wu.rearrange("(kc p) f -> p kc f", p=128)
    for sl in range(NFF // 2):
        n = B.alt("wgu", [0, 1, 2])
        wgs, wus = V["wgs"][n], V["wus"][n]
        kwg, kwu = "wgs%d" % n, "wus%d" % n
        if not _NODMA:
            B.dma("pool", wgs, wgv[:, :, sl * 256:(sl + 1) * 256], ["W" + uid], [kwg])
            B.dma("pool", wus, wuv[:, :, sl * 256:(sl + 1) * 256], ["W" + uid], [kwu])
        for j in range(2):
            ff = sl * 2 + j
            pb = B.alt("gubank", [0, 1])
            gps, ups = C["banks"][2 * pb], C["banks"][2 * pb + 1]
            kgp, kup = "bk%d" % (2 * pb), "bk%d" % (2 * pb + 1)
            for kc in range(16):
                B.mm(gps[:], wgs[:, kc, j * 128:(j + 1) * 128], xnT[:, kc, :], kc == 0, kc == 15,
                     [kwg, "xnT"], [kgp])
            for kc in range(16):
                B.mm(ups[:], wus[:, kc, j * 128:(j + 1) * 128], xnT[:, kc, :], kc == 0, kc == 15,
                     [kwu, "xnT"], [kup])
            ksg = "sg%d" % pb
            B.act(V["sgs"][pb], gps[:], AF.Silu, [kgp], [ksg])
            B.v("dve", "tensor_tensor", [ksg, kup], ["act%d" % ff], out=act[:, ff, :], in0=V["sgs"][pb],
                in1=ups[:], op=ALU.mult)


def ffn_down(B, C, V, src, dst, ksrc, kdst, wd, uid, hook=None):
    act = V["act"]
    wdv = wd.rearrange("(f p) d -> p f d", p=128)
    accs = C["banks"][4:8]
    nfg = NFF // 4
    for s in range(4):
        xs_ids = []
        for tt in range(4):
            n = B.alt("xs", [0, 1, 2, 3])
            xs_ids.append(n)
            B.dma("sp", V["xss"][n], src[tt * 128:(tt + 1) * 128, s * 512:(s + 1) * 512], [ksrc], ["xs%d" % n])
        for fg in range(nfg):
            n = B.alt("wd", [0, 1, 2])
            wds = V["wds"][n]
            kwd = "wds%d" % n
            if not _NODMA:
                B.dma("pool", wds, wdv[:, fg * 4:(fg + 1) * 4, s * 512:(s + 1) * 512], ["W" + uid], [kwd])
            if fg < nfg - 1:
                order = [(j, tt) for j in range(4) for tt in range(4)]
            else:
                order = [(j, tt) for tt in range(4) for j in range(4)]
            for (j, tt) in order:
                ff = fg * 4 + j
                B.mm(accs[tt][:], act[:, ff, tt * 128:(tt + 1) * 128], wds[:, j, :], ff == 0,
                     ff == NFF - 1, [kwd, "act%d" % ff], ["bk%d" % (4 + tt)])
        for tt in range(4):
            n = xs_ids[tt]
            kxs, kos = "xs%d" % n, "os%d" % n
            B.v("dve", "scalar_tensor_tensor", ["bk%d" % (4 + tt), kxs], [kos], out=V["oss"][n],
                in0=accs[tt][:], scalar=0.5, in1=V["xss"][n], op0=ALU.mult, op1=ALU.add)
            B.dma("sp", dst[tt * 128:(tt + 1) * 128, s * 512:(s + 1) * 512], V["oss"][n], [kos], [kdst])
        if hook is not None:
            hook(s)


def ffn_block_run(B, C, src, dst, ksrc, kdst, g_bc, kg, wg, wu, wd, uid):
    V = ffn_views(C)
    chunk = lambda ap, c: ap[c * 512:(c + 1) * 512, :]
    ffn_prep_norm(B, C, V, chunk(src, 0), ksrc[0], [0, 1], g_bc, kg)
    ffn_prep_tr(B, C, V, [0, 1])
    ffn_prep_norm(B, C, V, chunk(src, 0), ksrc[0], [2, 3], g_bc, kg)
    ffn_prep_tr(B, C, V, [2, 3])
    for c in range(4):
        ffn_gate_up(B, C, V, wg, wu, uid)
        hook = None
        if c + 1 < 4:
            nsrc, nk = chunk(src, c + 1), ksrc[c + 1]
            ffn_prep_norm(B, C, V, nsrc, nk, [0, 1], g_bc, kg)

            def hook(s, nsrc=nsrc, nk=nk):
                if s == 0:
                    ffn_prep_tr(B, C, V, [0, 1])
                    ffn_prep_norm(B, C, V, nsrc, nk, [2, 3], g_bc, kg)
                elif s == 2:
                    ffn_prep_tr(B, C, V, [2, 3])
        ffn_down(B, C, V, chunk(src, c), chunk(dst, c), ksrc[c], kdst[c], wd, uid, hook)


def attn_run(B, C, blocks, depth=2):
    groups = [blocks[i:i + 4] for i in range(0, len(blocks), 4)]
    pTs = C["pT"]
    nper, seen = {}, {}
    for b_ in blocks:
        nper[b_["ob"]] = nper.get(b_["ob"], 0) + 1

    def scores(gi):
        sbk = B.alt("scbank", SC_BANKS)
        bank = C["banks"][sbk]
        kb = "bk%d" % sbk
        grp = groups[gi]
        mms = []
        for bi, b_ in enumerate(grp):
            cs = slice(bi * 128, (bi + 1) * 128)
            mms.append((bank[:, cs], b_["kT"], b_["qT"], b_["rk"]))
            for (l, r, rk) in b_["extras"]:
                mms.append((bank[:, cs], l, r, rk))
        for i, (o, l, r, rk) in enumerate(mms):
            B.mm(o, l, r, i == 0, i == len(mms) - 1, rk, [kb])
        pi = B.alt("pT", [0, 1, 2])
        kp = "pT%d" % pi
        n = len(grp) * 128
        B.act(pTs[pi][:, 0:n], bank[:, 0:n], AF.Exp, [kb], [kp], scale=0.125)
        return pTs[pi], kp

    def pv(gi, pT, kp):
        for bi, b_ in enumerate(groups[gi]):
            c0, ncol = b_["oc"]
            ob = b_["ob"]
            k = seen.get(ob, 0)
            seen[ob] = k + 1
            B.mm(C["banks"][ob][:, c0:c0 + ncol], pT[:, bi * 128:(bi + 1) * 128], b_["v"], k == 0,
                 k == nper[ob] - 1, [kp] + b_["rk"], ["bk%d" % ob])

    pend = {}
    nxt = 0
    for gi in range(len(groups)):
        while nxt < len(groups) and nxt <= gi + depth:
            pend[nxt] = scores(nxt)
            nxt += 1
        pT, kp = pend.pop(gi)
        pv(gi, pT, kp)


def gelu_tanh(B, eng_keys, x, kx, tmp, kt, out, kout):
    B.v("dve", "tensor_tensor", [kx], [kt], out=tmp, in0=x, in1=x, op=ALU.mult)
    B.v("dve", "tensor_scalar", [kt], [kt], out=tmp, in0=tmp, scalar1=0.044715, scalar2=1.0,
        op0=ALU.mult, op1=ALU.add)
    B.v("dve", "tensor_tensor", [kt, kx], [kt], out=tmp, in0=tmp, in1=x, op=ALU.mult)
    B.act(tmp, tmp, AF.Sigmoid, [kt], [kt], scale=1.5957691216057308)
    B.v("dve", "tensor_tensor", [kt, kx], [kout], out=out, in0=tmp, in1=x, op=ALU.mult)


def mixer(B, C, L, src, dst, ksrc, kdst, W, K, SCR, stop=99, dbg=None):
    A = C["arena"]
    banks = C["banks"]
    uid = "m%d" % L
    vE = A.view(0, [16, 6, 66], BF16)
    gl = A.view(12672, [16, 24], F32)
    hT = A.view(14208, [16, 2048], BF16)
    yT = hT
    WS = 79744
    qkT, rgT = SCR["qkT"], SCR["rgT"]

    if stop <= -1:
        return
    pv_ = C["pvec"]
    B.dma("sp", pv_[:], W["pvec"][L], [], ["pvec"])
    B.dma("sp", C["esink"][:], W["sinks"][L], [], ["esink"])
    B.act(C["esink"][:], C["esink"][:], AF.Exp, ["esink"], ["esink"])
    gmix, kgm = load_gain(B, C, W["mix_norm"][L:L + 1, :], 0)
    ggn, kgg = load_gain(B, C, W["group_norm"][L:L + 1, 0:1024], 1)
    B.v("dve", "memset", [], ["vE"], vE[:, :, :, 64:65], 1.0)

    for t in range(16):
        xt = C["xt"][t % 2]
        kx = "xt%d" % (t % 2)
        B.dma("sp", xt[:], src[t * 128:(t + 1) * 128, :], [ksrc[t // 4]], [kx])
        norm_T(B, C, xt[:], kx, 2048, gmix[:, 0:2048], kgm, hT[:, :, t * 128:(t + 1) * 128], "hT", t % 2)

    if stop <= 0:
        return
    slabs = [A.view(WS + i * 16384, [16, 512], BF16) for i in range(2)]
    stg = [A.view(WS + 32768 + i * 2048, [512], F32) for i in range(4)]
    stgb = [A.view(WS + 32768 + i * 2048, [512], BF16) for i in range(4)]
    wqk = W["w_in_qk"][L].rearrange("(kc p) f -> p kc f", p=128)
    wrg = W["w_in"][L].rearrange("(kc p) f -> p kc f", p=128)
    wtm = W["w_in_tm"][L].rearrange("(kc p) f -> p kc f", p=128)
    import os as _os
    for si in range(int(_os.environ.get('M2_SLABS', '10'))):
        n = B.alt("wsl", [0, 1])
        ksl = "wsl%d" % n
        if si < 6:
            B.dma("pool", slabs[n], wqk[:, :, si * 512:(si + 1) * 512], [], [ksl])
        else:
            c0 = 2072 + (si - 6) * 512
            B.dma("pool", slabs[n], wrg[:, :, c0:c0 + 512], [], [ksl])
        for j in range(4):
            ch = si * 4 + j if si < 6 else (si - 6) * 4 + j
            if si == 5 and j >= 2:
                continue
            for tc in range(4):
                pb = B.alt("pjbank", [0, 1, 2, 3, 4, 5])
                kb = "bk%d" % pb
                for kc in range(16):
                    B.mm(banks[pb][:], slabs[n][:, kc, j * 128:(j + 1) * 128], hT[:, kc, tc * 512:(tc + 1) * 512],
                         kc == 0, kc == 15, [ksl, "hT"], [kb])
                sn = B.alt("stg", [0, 1, 2, 3])
                kst = "stg%d" % sn
                dst_sb = stgb[sn] if si < 6 else stg[sn]
                if B.alt("pjcopy", ["act", "dve"]) == "act":
                    B.act(dst_sb, banks[pb][:], AF.Copy, [kb], [kst])
                else:
                    B.v("dve", "tensor_copy", [kb], [kst], out=dst_sb, in_=banks[pb][:])
                if si < 6:
                    B.dma("sp", qkT[ch * 128:(ch + 1) * 128, tc * 512:(tc + 1) * 512], dst_sb, [kst], ["qk%d" % ch])
                else:
                    B.dma("sp", rgT[ch * 128:(ch + 1) * 128, tc * 512:(tc + 1) * 512], dst_sb, [kst], ["rg%d" % ch])
    n = B.alt("wsl", [0, 1])
    ksl = "wsl%d" % n
    if not _os.environ.get("TM_SKIP_DMA"):
        B.dma("pool", slabs[n][:, :, 0:408], wtm, [], [ksl])
    for t in range(int(_os.environ.get('M2_TM', '16'))):
        pb = B.alt("pjbank", [0, 1, 2, 3, 4, 5])
        kb = "bk%d" % pb
        for kc in range(16):
            B.mm(banks[pb][:, 0:408], hT[:, kc, t * 128:(t + 1) * 128], slabs[n][:, kc, 0:408], kc == 0, kc == 15,
                 [ksl, "hT"], [kb])
        if not _os.environ.get("TM_SKIP_ACT"):
            B.act(vE[:, t, :, 0:64], banks[pb][:, 0:384].rearrange("p (s d) -> p s d", s=6), AF.Copy, [kb], ["vE"])
        if not _os.environ.get("TM_SKIP_GL"):
            B.v("dve", "tensor_copy", [kb], ["gl"], out=gl[:, t, :], in_=banks[pb][:, 384:408])

    if stop <= 1:
        return
    B.barrier()
    f32v = [A.view(WS + i * 8192, [2048], F32) for i in range(6)]
    xcb = A.view(WS + 49152, [2048], BF16)
    wab = A.view(WS + 53248, [2, 128], BF16)
    sqlo = A.view(WS + 53760, [2048], BF16)
    oneb = C["oneb"]
    ssqC = banks[7]
    rgv = rgT.rearrange("(c p) t -> p c t", p=128)
    PVOFF = {"cw0": 0, "cw1": 8, "cw2": 16, "cw3": 24, "cb": 32, "ba": 40, "bx": 48, "lam": 56, "gc": 64}
    PVC = lambda name, cc: pv_[:, PVOFF[name] + cc:PVOFF[name] + cc + 1]
    nsp = C["nsp8"]
    B.act(nsp[:], pv_[:, 56:64], AF.Exp, ["pvec"], ["nsp8"], scale=-1.0)
    B.act(nsp[:], nsp[:], AF.Ln, ["nsp8", "onec"], ["nsp8"], bias=C["one"][:], scale=1.0)
    B.v("dve", "tensor_scalar", ["nsp8"], ["nsp8"], out=nsp[:], in0=nsp[:], scalar1=-8.0, scalar2=None, op0=ALU.mult)
    for cc in range(8):
        xr, xg, xc, r_, i_, t_ = f32v
        kxr, kxg, kxc, kr_, ki_, kt_ = ["f32v%d" % i for i in range(6)]
        B.dma("sp", xr, rgv[:, cc, :], ["rg%d" % cc], [kxr])
        B.dma("sp", xg, rgv[:, 8 + cc, :], ["rg%d" % (8 + cc)], [kxg])
        B.dma("pool", wab[:, 0, :], W["wabd"][L, cc], [], ["wab"])
        B.dma("pool", wab[:, 1, :], W["wxbd"][L, cc], [], ["wab"])
        B.v("dve", "tensor_scalar", [kxr, "pvec"], [kxc], out=xc, in0=xr, scalar1=PVC("cw3", cc),
            scalar2=PVC("cb", cc), op0=ALU.mult, op1=ALU.add)
        for j in range(3):
            sh = 3 - j
            B.v("dve", "scalar_tensor_tensor", [kxr, kxc, "pvec"], [kxc], out=xc[:, sh:], in0=xr[:, 0:S - sh],
                scalar=PVC("cw%d" % j, cc), in1=xc[:, sh:], op0=ALU.mult, op1=ALU.add)
        B.act(xcb, xc, AF.Copy, [kxc], ["xcb"])
        for gi, (dstg, kdg, bname) in enumerate([(r_, kr_, "ba"), (i_, ki_, "bx")]):
            for tc in range(4):
                pb = B.alt("cgbank", [0, 1, 2, 3])
                kb = "bk%d" % pb
                B.mm(banks[pb][:], wab[:, gi, :], xcb[:, tc * 512:(tc + 1) * 512], True, True, ["wab", "xcb"], [kb])
                B.act(dstg[:, tc * 512:(tc + 1) * 512], banks[pb][:], AF.Sigmoid, [kb, "pvec"], [kdg],
                      bias=PVC(bname, cc), scale=1.0)
        B.act(r_, r_, AF.Exp, [kr_, "nsp8"], [kr_], scale=nsp[:, cc:cc + 1])
        B.v("dve", "tensor_tensor", [kr_], [kt_], out=t_, in0=r_, in1=r_, op=ALU.mult)
        B.v("dve", "tensor_scalar", [kt_], [kt_], out=t_, in0=t_, scalar1=-1.0, scalar2=1.0, op0=ALU.mult, op1=ALU.add)
        B.act(t_, t_, AF.Sqrt, [kt_], [kt_])
        B.v("dve", "tensor_tensor", [kt_, ki_], [kt_], out=t_, in0=t_, in1=i_, op=ALU.mult)
        B.v("dve", "tensor_tensor", [kt_, kxc], [kt_], out=t_, in0=t_, in1=xc, op=ALU.mult)
        B.v("dve", "tensor_tensor_scan", [kr_, kt_], [ki_], out=i_, data0=r_, data1=t_, initial=0.0,
            op0=ALU.mult, op1=ALU.add)
        gelu_tanh(B, None, xg, kxg, t_, kt_, xr, kxr)
        B.v("dve", "tensor_tensor", [kxr, ki_], [kxc], out=xc, in0=xr, in1=i_, op=ALU.mult)
        B.act(t_, xc, AF.Square, [kxc], [kt_])
        B.v("dve", "tensor_copy", [kt_], ["xcb"], out=xcb, in_=t_)
        B.v("dve", "tensor_tensor", [kt_, "xcb"], [kt_], out=t_, in0=t_, in1=xcb, op=ALU.subtract)
        B.v("dve", "tensor_copy", [kt_], ["sqlo"], out=sqlo, in_=t_)
        for t in range(16):
            col = cc * 16 + t
            B.mm(ssqC[:, col:col + 1], xcb[:, t * 128:(t + 1) * 128], oneb[:, 0:1], True, False,
                 ["xcb", "onec"], ["bk7"])
            B.mm(ssqC[:, col:col + 1], sqlo[:, t * 128:(t + 1) * 128], oneb[:, 0:1], False, True,
                 ["sqlo", "onec"], ["bk7"])
        B.v("dve", "tensor_scalar", [kxc, "pvec"], ["yT"], out=yT[:, 8 + cc, :], in0=xc, scalar1=PVC("gc", cc),
            scalar2=None, op0=ALU.mult)
    rsC = C["rstdC"]
    B.v("dve", "tensor_reduce", ["bk7"], ["rstdC"], out=rsC[:], in_=ssqC[:, 0:128].rearrange("p (c t) -> p t c", c=8),
        axis=AX.X, op=ALU.add)
    B.act(rsC[:], rsC[:], AF.Sqrt, ["rstdC", "epsc"], ["rstdC"], bias=C["eps"][:], scale=1.0 / 1024)
    B.v("dve", "reciprocal", ["rstdC"], ["rstdC"], out=rsC[:], in_=rsC[:])

    if stop <= 2:
        return
    B.barrier()
    qkv = qkT.rearrange("(c p) t -> p c t", p=128)
    qaT = A.view(WS, [4, 2048], BF16)
    kaT = A.view(WS + 16384, [4, 2048], BF16)
    C["pT"] = [A.view(WS + 49152 + i * 1024, [512], BF16) for i in range(3)]
    ytile = [A.view(WS + 53248 + i * 2048, [8, 64], F32) for i in range(2)]
    B.dma("sp", qaT, qkv[:, 0:4, :], ["qk0", "qk1", "qk2", "qk3"], ["qaT"])
    B.dma("sp", kaT, qkv[:, 4:8, :], ["qk4", "qk5", "qk6", "qk7"], ["kaT"])
    identb, diagb, edgeb = K["identb"], K["diagb"], K["edgeb"]
    for qt in range(16):
        yt_ = ytile[qt % 2]
        kyt = "ytile%d" % (qt % 2)
        for kv in range(2):
            blocks = []
            kts = [qt - 1, qt] if qt > 0 else [qt]
            for g in range(4):
                h = kv * 4 + g
                base = 64 * (h % 2)
                for kt in kts:
                    mb = diagb if kt == qt else edgeb
                    blocks.append(dict(
                        kT=kaT[:, kv * 2 + h % 2, kt * 128:(kt + 1) * 128],
                        qT=qaT[:, h // 2, qt * 128:(qt + 1) * 128],
                        extras=[(identb[:], mb[:], ["kconst"])],
                        v=vE[:, kt, kv, 0:65], oc=(g * 65, 65), rk=["qaT", "kaT", "vE"], ob=0))
            ob = B.alt("obank", [3, 4, 5])
            kob = "bk%d" % ob
            for b_ in blocks:
                b_["ob"] = ob
            attn_run(B, C, blocks)
            ov = banks[ob][:, 0:260].rearrange("p (g c) -> p g c", c=65)
            den = C["den"][0]
            B.v("dve", "tensor_tensor", [kob, "esink"], ["den0"], out=den[:], in0=ov[:, :, 64],
                in1=C["esink"][:, kv * 4:(kv + 1) * 4], op=ALU.add)
            B.v("dve", "reciprocal", ["den0"], ["den0"], out=den[:], in_=den[:])
            B.v("dve", "tensor_tensor", [kob, "den0"], [kyt], out=yt_[:, kv * 4:(kv + 1) * 4, :], in0=ov[:, :, 0:64],
                in1=den[:].unsqueeze(2).broadcast_to([128, 4, 64]), op=ALU.mult)
        norm_T(B, C, yt_.rearrange("p h d -> p (h d)"), kyt, 512, ggn[:, 0:512], kgg,
               yT[:, 0:4, qt * 128:(qt + 1) * 128], "yT", qt % 2)

    if stop <= 3:
        return
    B.barrier()
    qbT = A.view(WS, [4, 2048], BF16)
    ksT = A.view(WS + 16384, [4, 2048], BF16)
    kwT = A.view(WS + 32768, [4, 2048], BF16)
    w1sb = A.view(WS + 16384, [32, 256], BF16)
    cxT = A.view(WS + 32768, [2048], BF16)
    cxz = [A.view(WS + 36864 + i * 4096, [2048], BF16) for i in range(2)]
    ctmp = [A.view(WS + 45056 + i * 512, [128], F32) for i in range(3)]
    B.dma("sp", qbT, qkv[:, 8:12, :], ["qk8", "qk9", "qk10", "qk11"], ["qbT"])
    KcT, VcE, hTb, w2sb, posT = C["KcT"], C["VcE"], C["hTb"], C["w2sb"], C["posT"]
    B.v("dve", "memset", [], ["KcT"], KcT[:], 0.0)
    B.v("dve", "memset", [], ["VcE"], VcE[:], 0.0)
    B.v("dve", "memset", ["VcE"], ["VcE"], VcE[:, :, 64:65], 1.0)
    B.dma("pool", C["ovl"][:], K["overlap_d"], [], ["ovl"])
    for kv in range(2):
        B.v("dve", "tensor_copy", ["ovl", "VcE"], ["VcE"], out=VcE[:, kv, 65:97], in_=C["ovl"][:])
    for i in range(2):
        B.dma("sp", cxT, qkv[:, 20 + i, :], ["qk%d" % (20 + i)], ["cxT"])
        for kv in range(2):
            B.v("dve", "memset", [], ["cxz%d" % kv], cxz[kv], 0.0)
            B.v("dve", "tensor_copy", ["cxT"], ["cxz%d" % kv], out=cxz[kv][64 * kv:64 * kv + 64, :],
                in_=cxT[64 * kv:64 * kv + 64, :])
        w1v = W["w1r"][L, i].rearrange("p (a b) -> p a b", a=32)
        for q4 in range(4):
            B.dma("pool", w1sb[:, q4 * 8:(q4 + 1) * 8, :], w1v[:, q4 * 8:(q4 + 1) * 8, :], [], ["w1sb"])
        B.dma("pool", w2sb[:].rearrange("p a b -> p (a b)"), W["w2r"][L, i], [], ["w2sb"])
        B.dma("pool", posT[:], W["posT"][L, i], [], ["posT"])
        if _os.environ.get("NSA_STOP") == "dma":
            continue
        hb = banks[0]
        first = True
        for kv in range(2):
            base = 64 * kv
            for jc in range(2):
                blk = kv * 2 + jc
                for l in range(32):
                    if _os.environ.get("NSA_MM") != "nomain":
                        B.mm(hb[:, blk * 128:blk * 128 + 127], w1sb[:, l, jc * 128:(jc + 1) * 128],
                             cxz[kv][:, l:l + 16 * 126 + 1:16], first, False, ["w1sb", "cxz%d" % kv], ["bk0"])
                        first = False
                    if _os.environ.get("NSA_MM") == "nopos":
                        continue
                    B.mm(hb[:, blk * 128 + 127:blk * 128 + 128], w1sb[:, l, jc * 128:(jc + 1) * 128],
                         posT[:, l:l + 1], False, (blk == 3 and l == 31), ["w1sb", "posT"], ["bk0"])
        if _os.environ.get("NSA_STOP") == "h":
            continue
        for blk in range(4):
            jc = blk % 2
            cb = C["cb"]
            B.v("dve", "tensor_tensor", ["bk0", "pvec"], ["cb"], out=cb[:], in0=hb[:, blk * 128 + 127:blk * 128 + 128],
                in1=pv_[:, 72 + 2 * i + jc:72 + 2 * i + jc + 1], op=ALU.add)
            B.v("dve", "tensor_scalar", ["bk0", "cb"], ["ctmp0"], out=ctmp[0][:, 0:127], in0=hb[:, blk * 128:blk * 128 + 127],
                scalar1=cb[:], scalar2=None, op0=ALU.add)
            gelu_tanh(B, None, ctmp[0][:, 0:127], "ctmp0", ctmp[1][:, 0:127], "ctmp1", hTb[:, blk, 0:127], "hTb")
        if _os.environ.get("NSA_STOP") == "g":
            continue
        ob = banks[1]
        if i == 0:
            for kv in range(2):
                for jc in range(2):
                    B.mm(ob[:, kv * 128:kv * 128 + 127], w2sb[:, jc, :], hTb[:, kv * 2 + jc, 0:127],
                         kv == 0 and jc == 0, kv == 1 and jc == 1, ["w2sb", "hTb"], ["bk1"])
            for kv in range(2):
                for par in range(2):
                    B.v("dve", "tensor_copy", ["bk1"], ["KcT"], out=KcT[64 * par:64 * par + 64, kv * 2 + par, 0:127],
                        in_=ob[64 * par:64 * par + 64, kv * 128:kv * 128 + 127])
        else:
            for kv in range(2):
                for jc in range(2):
                    B.mm(ob[0:127, kv * 64:(kv + 1) * 64], hTb[:, kv * 2 + jc, 0:127], w2sb[:, jc, 0:64],
                         kv == 0 and jc == 0, kv == 1 and jc == 1, ["w2sb", "hTb"], ["bk1"])
            B.v("dve", "tensor_copy", ["bk1"], ["VcE"], out=VcE[0:127, :, 0:64],
                in_=ob[0:127, 0:128].rearrange("p (k d) -> p k d", k=2))
    if _os.environ.get("NSA_STOP") in ("cmp", "dma", "h", "g"):
        return
    B.barrier()
    B.dma("sp", ksT, qkv[:, 12:16, :], ["qk12", "qk13", "qk14", "qk15"], ["ksT"])
    B.dma("sp", kwT, qkv[:, 16:20, :], ["qk16", "qk17", "qk18", "qk19"], ["kwT"])
    cmaskb, Eexp = K["cmaskb"], K["Eexp"]
    selA, selB = K["selA"], K["selB"]
    sgt, imp, imp2, imp3, m8, m8b, negb, negT = (C[k] for k in
                                                  ["sgt", "imp", "imp2", "imp3", "m8", "m8b", "negb", "negT"])
    tmpM = C["tmpM"]
    for qt in range(16):
        yt_ = ytile[qt % 2]
        kyt = "ytile%d" % (qt % 2)
        B.act(sgt[:], gl[:, qt, :], AF.Sigmoid, ["gl"], ["sgt"])
        for kv in range(2):
            blocks = []
            for g in range(4):
                h = kv * 4 + g
                blocks.append(dict(
                    kT=KcT[:, kv * 2 + h % 2, :],
                    qT=qbT[:, h // 2, qt * 128:(qt + 1) * 128],
                    extras=[(identb[:], cmaskb[:, qt * 128:(qt + 1) * 128], ["kconst"])],
                    v=VcE[:, kv, 0:97], oc=(g * 97, 97), rk=["qbT", "KcT", "VcE"], ob=3))
            for g in range(4):
                h = kv * 4 + g
                for kt in range(max(0, qt - 4), qt + 1):
                    ex = []
                    if kt == qt:
                        ex.append((identb[:], diagb[:], ["kconst"]))
                    if kt == qt - 4:
                        ex.append((identb[:], edgeb[:], ["kconst"]))
                    blocks.append(dict(
                        kT=kwT[:, kv * 2 + h % 2, kt * 128:(kt + 1) * 128],
                        qT=qbT[:, h // 2, qt * 128:(qt + 1) * 128],
                        extras=ex, v=vE[:, kt, 4 + kv, 0:65], oc=(g * 65, 65), rk=["qbT", "kwT", "vE"], ob=5))
            attn_run(B, C, blocks)
            ovc = banks[3][:, 0:388].rearrange("p (g c) -> p g c", c=97)
            dc, ds, dw = C["den"][0], C["den"][1], C["den"][2]
            B.v("dve", "tensor_scalar", ["bk3"], ["den0"], out=dc[:], in0=ovc[:, :, 64], scalar1=1e-30, scalar2=None,
                op0=ALU.add)
            B.v("dve", "reciprocal", ["den0"], ["den0"], out=dc[:], in_=dc[:])
            B.v("dve", "tensor_tensor", ["bk3", "den0"], ["tmpM"], out=tmpM[:], in0=ovc[:, :, 65:97],
                in1=dc[:].unsqueeze(2).broadcast_to([128, 4, 32]), op=ALU.mult)
            B.v("dve", "tensor_reduce", ["tmpM"], ["imp"], out=imp[:], in_=tmpM[:].rearrange("p g j -> p j g"),
                axis=AX.X, op=ALU.add)
            B.v("dve", "tensor_tensor", ["imp", "kconst"], ["imp2"], out=imp2[:], in0=imp[:], in1=selA[:, qt, :], op=ALU.mult)
            B.v("dve", "tensor_tensor", ["imp2", "kconst"], ["imp2"], out=imp2[:], in0=imp2[:], in1=selB[:, qt, :], op=ALU.add)
            B.v("dve", "max", ["imp2"], ["m8"], out=m8[:], in_=imp2[:])
            B.v("dve", "match_replace", ["imp2", "m8"], ["imp3"], out=imp3[:], in_to_replace=m8[:], in_values=imp2[:],
                imm_value=-3.0e38)
            B.v("dve", "max", ["imp3"], ["m8b"], out=m8b[:], in_=imp3[:])
            B.v("dve", "tensor_scalar", ["imp2", "m8b"], ["negb"], out=negb[:], in0=imp2[:], scalar1=m8b[:, 7:8],
                scalar2=None, op0=ALU.is_ge)
            B.v("dve", "tensor_scalar", ["negb"], ["negb"], out=negb[:], in0=negb[:], scalar1=-NEG, scalar2=NEG,
                op0=ALU.mult, op1=ALU.add)
            B.tr(banks[6][0:32, 0:128], negb[:], C["identf"][:], ["negb", "ident"], ["bk6"])
            B.v("dve", "tensor_copy", ["bk6"], ["negT"], out=negT[0:32, :], in_=banks[6][0:32, 0:128])
            blocks = []
            for g in range(4):
                h = kv * 4 + g
                for kt in range(qt + 1):
                    ex = [(Eexp[:, kt, :], negT[:], ["kconst", "negT"])]
                    if kt == qt:
                        ex.append((identb[:], diagb[:], ["kconst"]))
                    blocks.append(dict(
                        kT=ksT[:, kv * 2 + h % 2, kt * 128:(kt + 1) * 128],
                        qT=qbT[:, h // 2, qt * 128:(qt + 1) * 128],
                        extras=ex, v=vE[:, kt, 2 + kv, 0:65], oc=(g * 65, 65), rk=["qbT", "ksT", "vE"], ob=4))
            attn_run(B, C, blocks)
            ovs = banks[4][:, 0:260].rearrange("p (g c) -> p g c", c=65)
            ovw = banks[5][:, 0:260].rearrange("p (g c) -> p g c", c=65)
            B.v("dve", "reciprocal", ["bk4"], ["den1"], out=ds[:], in_=ovs[:, :, 64])
            B.v("dve", "reciprocal", ["bk5"], ["den2"], out=dw[:], in_=ovw[:, :, 64])
            B.v("dve", "tensor_tensor", ["den0", "sgt"], ["den0"], out=dc[:], in0=dc[:], in1=sgt[:, kv * 4:kv * 4 + 4], op=ALU.mult)
            B.v("dve", "tensor_tensor", ["den1", "sgt"], ["den1"], out=ds[:], in0=ds[:], in1=sgt[:, 8 + kv * 4:8 + kv * 4 + 4], op=ALU.mult)
            B.v("dve", "tensor_tensor", ["den2", "sgt"], ["den2"], out=dw[:], in0=dw[:], in1=sgt[:, 16 + kv * 4:16 + kv * 4 + 4], op=ALU.mult)
            ysl = yt_[:, kv * 4:(kv + 1) * 4, :]
            ot = C["otmp"]
            B.v("dve", "tensor_tensor", ["bk3", "den0"], [kyt], out=ysl, in0=ovc[:, :, 0:64],
                in1=dc[:].unsqueeze(2).broadcast_to([128, 4, 64]), op=ALU.mult)
            B.v("dve", "tensor_tensor", ["bk4", "den1"], ["otmp"], out=ot[:], in0=ovs[:, :, 0:64],
                in1=ds[:].unsqueeze(2).broadcast_to([128, 4, 64]), op=ALU.mult)
            B.v("dve", "tensor_tensor", ["otmp", kyt], [kyt], out=ysl, in0=ysl, in1=ot[:], op=ALU.add)
            B.v("dve", "tensor_tensor", ["bk5", "den2"], ["otmp"], out=ot[:], in0=ovw[:, :, 0:64],
                in1=dw[:].unsqueeze(2).broadcast_to([128, 4, 64]), op=ALU.mult)
            B.v("dve", "tensor_tensor", ["otmp", kyt], [kyt], out=ysl, in0=ysl, in1=ot[:], op=ALU.add)
        norm_T(B, C, yt_.rearrange("p h d -> p (h d)"), kyt, 512, ggn[:, 512:1024], kgg,
               yT[:, 4:8, qt * 128:(qt + 1) * 128], "yT", qt % 2)

    if dbg is not None:
        B.dma("sp", dbg.rearrange("(c p) t -> p c t", p=128), yT, ["yT"], ["dbg"])
    if stop <= 4:
        return
    B.barrier()
    wos = [A.view(WS + i * 16384, [16, 512], BF16) for i in range(2)]
    xss = [A.view(WS + 32768 + i * 2048, [512], F32) for i in range(4)]
    oss = [A.view(WS + 40960 + i * 2048, [512], F32) for i in range(4)]
    wov = W["w_out"][L].rearrange("(kc p) d -> p kc d", p=128)
    for s in range(4):
        n = B.alt("wos", [0, 1])
        kwo = "wos%d" % n
        B.dma("pool", wos[n], wov[:, :, s * 512:(s + 1) * 512], [], [kwo])
        for tt in range(16):
            pa = B.alt("wobank", [0, 2, 4])
            pc = pa + 1
            for kc in range(8):
                B.mm(banks[pa][:], yT[:, kc, tt * 128:(tt + 1) * 128], wos[n][:, kc, :], kc == 0, kc == 7,
                     ["yT", kwo], ["bk%d" % pa])
            for kc in range(8, 16):
                B.mm(banks[pc][:], yT[:, kc, tt * 128:(tt + 1) * 128], wos[n][:, kc, :], kc == 8, kc == 15,
                     ["yT", kwo], ["bk%d" % pc])
            m = B.alt("xso", [0, 1, 2, 3])
            kxs, kos = "mxs%d" % m, "mos%d" % m
            B.dma("sp", xss[m], src[tt * 128:(tt + 1) * 128, s * 512:(s + 1) * 512], [ksrc[tt // 4]], [kxs])
            B.v("dve", "tensor_tensor", ["bk%d" % pa, kxs], [kxs], out=xss[m], in0=banks[pa][:], in1=xss[m], op=ALU.add)
            B.v("dve", "scalar_tensor_tensor", ["bk%d" % pc, kxs, "rstdC"], [kos], out=oss[m], in0=banks[pc][:],
                scalar=rsC[:, tt:tt + 1], in1=xss[m], op0=ALU.mult, op1=ALU.add)
            B.dma("sp", dst[tt * 128:(tt + 1) * 128, s * 512:(s + 1) * 512], oss[m], [kos], [kdst[tt // 4]])


def _dup(a, b):
    return list(range(a, b)) * 2


def _top(a, b):
    return list(range(a, b)) + [-1] * 64


def _bot(a, b):
    return [-1] * 64 + list(range(a, b))


QK_COLS = (list(range(0, 512)) + _top(512, 576) + _bot(512, 576) + _top(576, 640) + _bot(576, 640) +
           list(range(768, 1280)) + _top(1536, 1600) + _bot(1536, 1600) + _top(1600, 1664) + _bot(1600, 1664) +
           _top(1792, 1856) + _bot(1792, 1856) + _top(1856, 1920) + _bot(1856, 1920) +
           list(range(1280, 1408)) + list(range(1408, 1536)) + [-1] * 256)
NQK = len(QK_COLS)
TM_COLS = list(range(640, 768)) + list(range(1664, 1792)) + list(range(1920, 2048)) + list(range(2048, 2072))


def host_consts():
    k = np.arange(128)[:, None]
    q = np.arange(128)[None, :]
    c = {}
    c["identf_d"] = np.eye(128, dtype=np.float32)
    c["diag_d"] = np.where(k <= q, 0.0, NEG).astype(np.float32)
    c["edge_d"] = np.where(k > q, 0.0, NEG).astype(np.float32)
    cc = np.arange(128)[:, None]
    qq = np.arange(S)[None, :]
    c["cmask_d"] = np.where((cc < 127) & (16 * cc + 31 <= qq), 0.0, NEG).astype(np.float32)
    E = np.zeros((128, 16, 128), np.float32)
    for kt in range(16):
        for kk in range(128):
            E[2 * kt + kk // 64, kt, kk] = 1.0
    c["eexp_d"] = E.reshape(128, 2048)
    qpos = np.arange(S)[:, None]
    j = np.arange(32)[None, :]
    cur = qpos // 64
    forced = (j == 0) | (j == cur) | (j == cur - 1)
    valid = j * 64 <= qpos
    selA = np.where(forced | ~valid, 0.0, 1.0).astype(np.float32)
    selB = np.where(~valid, -1e30, np.where(forced, 1e4, 0.0)).astype(np.float32)
    c["selA_d"] = np.ascontiguousarray(selA.reshape(16, 128, 32).transpose(1, 0, 2)).reshape(128, 512)
    c["selB_d"] = np.ascontiguousarray(selB.reshape(16, 128, 32).transpose(1, 0, 2)).reshape(128, 512)
    cs = np.arange(127)[:, None] * 16
    ss = np.arange(32)[None, :] * 64
    ov = np.clip(np.minimum(cs + 32, ss + 64) - np.maximum(cs, ss), 0, None) / 32.0
    o = np.zeros((128, 32), np.float32)
    o[:127] = ov
    c["overlap_d"] = o
    return c


def host_layout(inp):
    f = lambda a: np.ascontiguousarray(np.asarray(a, dtype=np.float32))
    d = {}
    for kk in ["ffn1_w_gate", "ffn1_w_up", "ffn1_w_down", "ffn2_w_gate", "ffn2_w_up", "ffn2_w_down", "w_in",
               "w_out", "ffn1_norm", "ffn2_norm", "mix_norm", "group_norm"]:
        d[kk] = f(inp[kk])
    d["final_norm"] = f(inp["final_norm"]).reshape(1, D)
    w_in = d["w_in"]
    cols = np.asarray(QK_COLS)
    wqk = np.zeros((DEPTH, D, NQK), np.float32)
    wqk[:, :, cols >= 0] = w_in[:, :, cols[cols >= 0]]
    d["w_in_qk"] = wqk
    d["w_in_tm"] = f(w_in[:, :, TM_COLS])
    pvec = np.zeros((DEPTH, 128, NPV), np.float32)
    cw, cb = f(inp["conv_w"]), f(inp["conv_b"])
    for l in range(DEPTH):
        for j in range(4):
            pvec[l, :, j * 8:(j + 1) * 8] = cw[l, j].reshape(8, 128).T
        pvec[l, :, 32:40] = cb[l].reshape(8, 128).T
        pvec[l, :, 40:48] = f(inp["lru_ba"])[l].reshape(8, 128).T
        pvec[l, :, 48:56] = f(inp["lru_bx"])[l].reshape(8, 128).T
        pvec[l, :, 56:64] = f(inp["lru_lambda"])[l].reshape(8, 128).T
        pvec[l, :, 64:72] = d["group_norm"][l, 1024:].reshape(8, 128).T
        for i in range(2):
            pvec[l, :, 72 + 2 * i:74 + 2 * i] = f(inp["cmp_b1"])[l, i].reshape(2, 128).T
    d["pvec"] = pvec
    d["sinks"] = f(np.broadcast_to(f(inp["swa_sinks"])[:, None, :], (DEPTH, 128, 8)))
    pos = f(inp["cmp_pos"])
    posT = np.zeros((DEPTH, 2, 128, 32), np.float32)
    posT[:, :, 0:64, :] = pos.transpose(0, 1, 3, 2)
    d["posT"] = posT
    w1 = f(inp["cmp_w1"]).reshape(DEPTH, 2, 32, 64, 256).transpose(0, 1, 3, 2, 4)
    d["w1r"] = f(np.tile(w1, (1, 1, 2, 1, 1))).reshape(DEPTH, 2, 128, 32 * 256)
    w2 = f(inp["cmp_w2"]).reshape(DEPTH, 2, 2, 128, 64).transpose(0, 1, 3, 2, 4)
    d["w2r"] = f(np.tile(w2, (1, 1, 1, 1, 2))).reshape(DEPTH, 2, 128, 256)
    for nm, src in [("wabd", "lru_wa"), ("wxbd", "lru_wx")]:
        w = f(inp[src])
        bd = np.zeros((DEPTH, 8, 128, 128), np.float32)
        for cc in range(8):
            bd[:, cc, 0:64, 0:64] = w[:, 2 * cc]
            bd[:, cc, 64:128, 64:128] = w[:, 2 * cc + 1]
        d[nm] = bd
    d.update(host_consts())
    return d


IN_SHAPES = {
    "ffn1_w_gate": [DEPTH, D, DFF], "ffn1_w_up": [DEPTH, D, DFF], "ffn1_w_down": [DEPTH, DFF, D],
    "ffn2_w_gate": [DEPTH, D, DFF], "ffn2_w_up": [DEPTH, D, DFF], "ffn2_w_down": [DEPTH, DFF, D],
    "w_in": [DEPTH, D, DIN], "w_out": [DEPTH, D, D], "ffn1_norm": [DEPTH, D], "ffn2_norm": [DEPTH, D],
    "mix_norm": [DEPTH, D], "group_norm": [DEPTH, D], "final_norm": [1, D],
    "w_in_qk": [DEPTH, D, NQK], "w_in_tm": [DEPTH, D, 408], "pvec": [DEPTH, 128, NPV], "sinks": [DEPTH, 128, 8],
    "posT": [DEPTH, 2, 128, 32], "w1r": [DEPTH, 2, 128, 8192], "w2r": [DEPTH, 2, 128, 256],
    "wabd": [DEPTH, 8, 128, 128], "wxbd": [DEPTH, 8, 128, 128],
    "identf_d": [128, 128], "diag_d": [128, 128], "edge_d": [128, 128], "cmask_d": [128, 2048],
    "eexp_d": [128, 2048], "selA_d": [128, 512], "selB_d": [128, 512], "overlap_d": [128, 32],
}


def setup_common(B, W):
    C = {}
    C["identf"] = B.sb("identf", [128, 128], F32)
    C["eps"] = B.sb("eps", [128, 1], F32)
    C["one"] = B.sb("one", [128, 1], F32)
    C["onef"] = B.sb("onef", [128, 2], F32)
    C["oneb"] = B.sb("oneb", [128, 2], BF16)
    C["junk"] = B.sb("junk", [128, 2048], BF16)
    C["ssq"] = [B.sb("ssq%d" % i, [128, 1], F32) for i in range(2)]
    C["rstd"] = [B.sb("rstd%d" % i, [128, 1], F32) for i in range(2)]
    C["xn"] = [B.sb("xn%d" % i, [128, 2048], F32) for i in range(2)]
    C["xt"] = [B.sb("xt%d" % i, [128, 2048], F32) for i in range(2)]
    C["gbc"] = [B.sb("gbc0", [128, 2048], F32), B.sb("gbc1", [128, 1024], F32)]
    C["pvec"] = B.sb("pvec", [128, NPV], F32)
    C["esink"] = B.sb("esink", [128, 8], F32)
    C["nsp8"] = B.sb("nsp8", [128, 8], F32)
    C["rstdC"] = B.sb("rstdC", [128, 16], F32)
    C["den"] = [B.sb("den%d" % i, [128, 4], F32) for i in range(3)]
    C["KcT"] = B.sb("KcT", [128, 4, 128], BF16)
    C["VcE"] = B.sb("VcE", [128, 2, 98], BF16)
    C["hTb"] = B.sb("hTb", [128, 4, 128], BF16)
    C["w2sb"] = B.sb("w2sb", [128, 2, 128], BF16)
    C["posT"] = B.sb("posT", [128, 32], BF16)
    C["ovl"] = B.sb("ovl", [128, 32], BF16)
    C["cb"] = B.sb("cbias", [128, 1], F32)
    for nm, shp in [("sgt", [128, 24]), ("imp", [128, 32]), ("imp2", [128, 32]), ("imp3", [128, 32]),
                    ("m8", [128, 8]), ("m8b", [128, 8]), ("negb", [128, 32]), ("tmpM", [128, 4, 32]),
                    ("otmp", [128, 4, 64])]:
        C[nm] = B.sb(nm, shp, F32)
    C["negT"] = B.sb("negT", [128, 128], BF16)
    K = {}
    K["identb"] = B.sb("identb", [128, 128], BF16)
    K["diagb"] = B.sb("diagb", [128, 128], BF16)
    K["edgeb"] = B.sb("edgeb", [128, 128], BF16)
    K["cmaskb"] = B.sb("cmaskb", [128, 2048], BF16)
    K["Eexp"] = B.sb("Eexp", [128, 16, 128], BF16)
    K["selA"] = B.sb("selA", [128, 16, 32], F32)
    K["selB"] = B.sb("selB", [128, 16, 32], F32)
    K["overlap_d"] = W["overlap_d"]
    C["arena"] = Arena(B)
    C["banks"] = [B.ps("bank%d" % i, [128, 512], F32) for i in range(8)]
    B.dma("sp", C["identf"][:], W["identf_d"], [], ["ident"])
    B.dma("pool", K["identb"][:], W["identf_d"], [], ["kconst"])
    B.dma("pool", K["diagb"][:], W["diag_d"], [], ["kconst"])
    B.dma("pool", K["edgeb"][:], W["edge_d"], [], ["kconst"])
    B.dma("pool", K["cmaskb"][:], W["cmask_d"], [], ["kconst"])
    B.dma("pool", K["Eexp"][:].rearrange("p a b -> p (a b)"), W["eexp_d"], [], ["kconst"])
    B.dma("sp", K["selA"][:].rearrange("p a b -> p (a b)"), W["selA_d"], [], ["kconst"])
    B.dma("sp", K["selB"][:].rearrange("p a b -> p (a b)"), W["selB_d"], [], ["kconst"])
    B.v("dve", "memset", [], ["epsc"], C["eps"][:], RMS_EPS)
    B.v("dve", "memset", [], ["negT"], C["negT"][:], 0.0)
    B.v("dve", "memset", [], ["onec"], C["one"][:], 1.0)
    B.v("dve", "memset", [], ["onec"], C["onef"][:], 1.0)
    B.v("dve", "memset", [], ["onec"], C["oneb"][:], 1.0)
    return C, K


def final_norm(B, C, src, ksrc, dst, kdst, g_bc, kg):
    for t in range(16):
        xt = C["xt"][t % 2]
        kx = "xt%d" % (t % 2)
        B.dma("sp", xt[:], src[t * 128:(t + 1) * 128, :], [ksrc[t // 4]], [kx])
        xn, kn = norm_T(B, C, xt[:], kx, 2048, g_bc[:, 0:2048], kg, None, None, t % 2, transpose=False)
        B.dma("sp", dst[t * 128:(t + 1) * 128, :], xn[:], [kn], [kdst])


def build_kernel():
    B = Builder()
    nc = B.nc
    W = {}
    for nm, shp in IN_SHAPES.items():
        W[nm] = nc.dram_tensor(nm, shp, F32, kind="ExternalInput").ap()
    x = nc.dram_tensor("x", [S, D], F32, kind="ExternalInput").ap()
    out = nc.dram_tensor("out", [S, D], F32, kind="ExternalOutput").ap()
    xa = B.dram("xa", [S, D], F32)
    xb = B.dram("xb", [S, D], F32)
    SCR = {"qkT": B.dram("qkT", [NQK, 2048], BF16), "rgT": B.dram("rgT", [2048, 2048], F32)}
    C, K = setup_common(B, W)
    cur, kcur = x, ["x%d" % c for c in range(4)]
    pp = [(xa, "xa"), (xb, "xb")]
    nxt = 0

    def ffn_block(pre, L, cur, kcur, dst, kd):
        g, kg = load_gain(B, C, W[pre + "_norm"][L:L + 1, :], 0)
        ffn_block_run(B, C, cur, dst, kcur, kd, g, kg, W[pre + "_w_gate"][L], W[pre + "_w_up"][L],
                      W[pre + "_w_down"][L], "%s_%d" % (pre, L))
        B.barrier()

    for L in range(DEPTH):
        dst, kd = pp[nxt][0], ["%s_%d_%d_%d" % (pp[nxt][1], L, 0, c) for c in range(4)]
        ffn_block("ffn1", L, cur, kcur, dst, kd)
        cur, kcur, nxt = dst, kd, 1 - nxt
        dst, kd = pp[nxt][0], ["%s_%d_%d_%d" % (pp[nxt][1], L, 1, c) for c in range(4)]
        mixer(B, C, L, cur, dst, kcur, kd, W, K, SCR)
        B.barrier()
        cur, kcur, nxt = dst, kd, 1 - nxt
        dst, kd = pp[nxt][0], ["%s_%d_%d_%d" % (pp[nxt][1], L, 2, c) for c in range(4)]
        ffn_block("ffn2", L, cur, kcur, dst, kd)
        cur, kcur, nxt = dst, kd, 1 - nxt
    g, kg = load_gain(B, C, W["final_norm"], 0)
    final_norm(B, C, cur, kcur, out, "out", g, kg)
    B.S.add("sp", None, ["out"], [])
    B.S.emit()
    return B


_CACHE = {}


def kernel(**inputs):
    x = np.ascontiguousarray(np.asarray(inputs["x"], dtype=np.float32))
    shared = host_layout(inputs)
    if "B" not in _CACHE:
        _CACHE["B"] = build_kernel()
    B = _CACHE["B"]
    in_maps = []
    for c in range(8):
        m = {nm: shared[nm] for nm in IN_SHAPES}
        m["x"] = x[c]
        in_maps.append(m)
    res = run_bass_kernel_spmd(B.nc, in_maps, core_ids=list(range(8)))
    return np.stack([np.asarray(r["out"], dtype=np.float32).reshape(S, D) for r in res.results], axis=0)
```
